# Optimizing a Trainium2 kernel written in Bass

```python
import math
import jax, jax.numpy as jnp
from jax import lax
import numpy as np

D_MODEL = 2048
BATCH = 2
SEQ = 4096
DEPTH = 1

MEM_LEN = 256
MIX_A_WIDTH = D_MODEL // 2
MIX_B_WIDTH = D_MODEL - MIX_A_WIDTH
A_GROUPS = 8
A_HEAD = MIX_A_WIDTH // A_GROUPS
A_CHUNK = 128
B_HEAD_K = 128
B_HEAD_V = 128
B_HEADS = MIX_B_WIDTH // B_HEAD_V
B_CHUNK = 64
XA_HEADS = 4
XA_HEAD = D_MODEL // XA_HEADS
PEER_HEADS = 8
PEER_NKEYS = 128
PEER_EXPERTS = PEER_NKEYS * PEER_NKEYS
PEER_QDIM = 256
PEER_HALF = PEER_QDIM // 2
PEER_TOPK = 16
PEER_TOKEN_BLOCK = 128
LN_EPS = 1e-5
DN_ALPHA = (2.0 * DEPTH) ** 0.25
DN_BETA = (8.0 * DEPTH) ** -0.25
IN_COLS = 2 * MIX_A_WIDTH + 4 * MIX_B_WIDTH
SPLITS = [MIX_A_WIDTH, 2 * MIX_A_WIDTH, 2 * MIX_A_WIDTH + MIX_B_WIDTH,
          2 * MIX_A_WIDTH + 2 * MIX_B_WIDTH, 2 * MIX_A_WIDTH + 3 * MIX_B_WIDTH]

kernel_name = 'hybrid_sgu_hgrn2_peer_deepnorm'


def layer_norm(x, g, b):
    xf = x.astype(jnp.float32)
    mu = jnp.mean(xf, axis=-1, keepdims=True)
    var = jnp.mean(jnp.square(xf - mu), axis=-1, keepdims=True)
    return ((xf - mu) * lax.rsqrt(var + LN_EPS) * g + b).astype(x.dtype)


def rms_norm(x, g):
    xf = x.astype(jnp.float32)
    return xf * lax.rsqrt(jnp.mean(jnp.square(xf), axis=-1, keepdims=True) + LN_EPS) * g


def chunked_sgu(u, v, w_s, b_s, vg, vb):
    B_, S_, _ = u.shape
    nc = S_ // A_CHUNK
    u = jax.nn.gelu(u)
    v = jax.nn.gelu(v).reshape(B_, S_, A_GROUPS, A_HEAD)
    v = layer_norm(v, vg, vb).reshape(B_, nc, A_CHUNK, A_GROUPS, A_HEAD)
    causal = jnp.tril(jnp.ones((A_CHUNK, A_CHUNK), dtype=bool))
    w = jnp.where(causal, w_s, 0.0)
    z = jnp.einsum('gts,bcsgd->bctgd', w, v) + b_s.T[None, None, :, :, None]
    return u * z.reshape(B_, S_, MIX_A_WIDTH)


def hgrn2(q, f_logit, i, g, lb, gn):
    B_, S_, _ = q.shape
    nc = S_ // B_CHUNK
    f32 = jnp.float32
    lbf = lb.astype(f32)
    f = lbf + (1.0 - lbf) * jax.nn.sigmoid(f_logit.astype(f32))
    log_f = jnp.log(f)
    k = 1.0 - f

    def to_chunks(t, d):
        return t.reshape(B_, nc, B_CHUNK, B_HEADS, d).transpose(1, 0, 3, 2, 4)

    qc = to_chunks(q.astype(f32), B_HEAD_K)
    kc = to_chunks(k, B_HEAD_K)
    lc = to_chunks(log_f, B_HEAD_K)
    ic = to_chunks(i.astype(f32), B_HEAD_V)
    causal = jnp.tril(jnp.ones((B_CHUNK, B_CHUNK), dtype=bool))[:, :, None]

    def step(state, xs):
        qb, kb, lfb, ib = xs
        a = jnp.cumsum(lfb, axis=2)
        a_last = a[:, :, -1, :]
        diff = a[:, :, :, None, :] - a[:, :, None, :, :]
        decay = jnp.exp(jnp.where(causal, diff, -jnp.inf))
        scores = jnp.einsum('bhtk,bhsk,bhtsk->bhts', qb, kb, decay)
        o = (jnp.einsum('bhts,bhsv->bhtv', scores, ib)
             + jnp.einsum('bhtk,bhkv->bhtv', qb * jnp.exp(a), state))
        new_state = (jnp.exp(a_last)[..., None] * state
                     + jnp.einsum('bhsk,bhsv->bhkv', kb * jnp.exp(a_last[:, :, None, :] - a), ib))
        return new_state, o

    s0 = jnp.zeros((B_, B_HEADS, B_HEAD_K, B_HEAD_V), f32)
    _, oc = lax.scan(step, s0, (qc, kc, lc, ic))
    o = oc.transpose(1, 0, 3, 2, 4).reshape(B_, S_, B_HEADS, B_HEAD_V)
    o = rms_norm(o, gn).reshape(B_, S_, MIX_B_WIDTH) * jax.nn.silu(g.astype(f32))
    return o.astype(q.dtype)


def memory_cross_attention(x, mem, wq, wk, wv, wo):
    B_, S_, _ = x.shape
    M_ = mem.shape[1]
    q = (x @ wq).reshape(B_, S_, XA_HEADS, XA_HEAD)
    k = (mem @ wk).reshape(B_, M_, XA_HEADS, XA_HEAD)
    v = (mem @ wv).reshape(B_, M_, XA_HEADS, XA_HEAD)
    s = jnp.einsum('bshd,bmhd->bhsm', q, k).astype(jnp.float32) * (1.0 / math.sqrt(XA_HEAD))
    p = jax.nn.softmax(s, axis=-1).astype(v.dtype)
    o = jnp.einsum('bhsm,bmhd->bshd', p, v).reshape(B_, S_, D_MODEL)
    return o @ wo


def peer(x, wq, sub_k1, sub_k2, u_tab, v_tab):
    B_, S_, _ = x.shape
    f32 = jnp.float32
    q = (x @ wq).astype(f32).reshape(B_, S_, PEER_HEADS, 2, PEER_HALF)
    s1 = jnp.einsum('bshd,nd->bshn', q[..., 0, :], sub_k1.astype(f32))
    s2 = jnp.einsum('bshd,nd->bshn', q[..., 1, :], sub_k2.astype(f32))
    v1, i1 = lax.top_k(s1, PEER_TOPK)
    v2, i2 = lax.top_k(s2, PEER_TOPK)
    cand = (v1[..., :, None] + v2[..., None, :]).reshape(B_, S_, PEER_HEADS, PEER_TOPK * PEER_TOPK)
    cand_idx = (i1[..., :, None] * PEER_NKEYS + i2[..., None, :]).reshape(B_, S_, PEER_HEADS, PEER_TOPK * PEER_TOPK)
    top_s, top_pos = lax.top_k(cand, PEER_TOPK)
    experts = jnp.take_along_axis(cand_idx, top_pos, axis=-1)
    gates = jax.nn.softmax(top_s, axis=-1).astype(x.dtype)
    nb = (B_ * S_) // PEER_TOKEN_BLOCK
    xb = x.reshape(nb, PEER_TOKEN_BLOCK, D_MODEL)
    eb = experts.reshape(nb, PEER_TOKEN_BLOCK, PEER_HEADS, PEER_TOPK)
    gb = gates.reshape(nb, PEER_TOKEN_BLOCK, PEER_HEADS, PEER_TOPK)

    def block(args):
        xt, et, gt = args
        u = u_tab[et]
        h = jax.nn.gelu(jnp.einsum('td,thkd->thk', xt, u))
        return jnp.einsum('thk,thkd->td', gt * h, v_tab[et])

    out = lax.map(block, (xb, eb, gb))
    return out.reshape(B_, S_, D_MODEL)


def setup_inputs(seed: int = 0) -> dict:
    key = jax.random.key(seed)
    ks = jax.random.split(key, 32)
    n = jax.random.normal
    L = DEPTH
    D = D_MODEL
    return {
        'x': n(ks[0], (BATCH, SEQ, D), jnp.float32),
        'mem': n(ks[1], (BATCH, MEM_LEN, D), jnp.float32),
        'w_in': n(ks[2], (L, D, IN_COLS), jnp.float32) * D ** -0.5,
        'sgu_w': n(ks[3], (L, A_GROUPS, A_CHUNK, A_CHUNK), jnp.float32) * A_CHUNK ** -0.5,
        'sgu_b': 1.0 + 0.02 * n(ks[4], (L, A_GROUPS, A_CHUNK), jnp.float32),
        'sgu_ln_g': 1.0 + 0.02 * n(ks[5], (L, A_GROUPS, A_HEAD), jnp.float32),
        'sgu_ln_b': 0.02 * n(ks[6], (L, A_GROUPS, A_HEAD), jnp.float32),
        'hgrn_lb_logits': 0.5 * n(ks[7], (L + 1, B_HEADS * B_HEAD_K), jnp.float32),
        'hgrn_norm_g': 1.0 + 0.02 * n(ks[8], (L, B_HEADS, B_HEAD_V), jnp.float32),
        'w_out': n(ks[9], (L, D, D), jnp.float32) * D ** -0.5 * DN_BETA,
        'ln1_g': 1.0 + 0.02 * n(ks[10], (L, D), jnp.float32),
        'ln1_b': 0.02 * n(ks[11], (L, D), jnp.float32),
        'xa_wq': n(ks[12], (L, D, D), jnp.float32) * D ** -0.5,
        'xa_wk': n(ks[13], (L, D, D), jnp.float32) * D ** -0.5,
        'xa_wv': n(ks[14], (L, D, D), jnp.float32) * D ** -0.5 * DN_BETA,
        'xa_wo': n(ks[15], (L, D, D), jnp.float32) * D ** -0.5 * DN_BETA,
        'ln2_g': 1.0 + 0.02 * n(ks[16], (L, D), jnp.float32),
        'ln2_b': 0.02 * n(ks[17], (L, D), jnp.float32),
        'peer_wq': n(ks[18], (L, D, PEER_HEADS * PEER_QDIM), jnp.float32) * D ** -0.5,
        'peer_k1': n(ks[19], (L, PEER_NKEYS, PEER_HALF), jnp.float32) * PEER_HALF ** -0.5,
        'peer_k2': n(ks[20], (L, PEER_NKEYS, PEER_HALF), jnp.float32) * PEER_HALF ** -0.5,
        'peer_u': n(ks[21], (L, PEER_EXPERTS, D), jnp.float32) * D ** -0.5,
        'peer_v': n(ks[22], (L, PEER_EXPERTS, D), jnp.float32) * DN_BETA * PEER_HEADS ** -0.5,
        'ln3_g': 1.0 + 0.02 * n(ks[23], (L, D), jnp.float32),
        'ln3_b': 0.02 * n(ks[24], (L, D), jnp.float32),
    }


def reference(x, mem, w_in, sgu_w, sgu_b, sgu_ln_g, sgu_ln_b, hgrn_lb_logits, hgrn_norm_g,
              w_out, ln1_g, ln1_b, xa_wq, xa_wk, xa_wv, xa_wo, ln2_g, ln2_b,
              peer_wq, peer_k1, peer_k2, peer_u, peer_v, ln3_g, ln3_b):
    lb_all = jnp.cumsum(jax.nn.softmax(hgrn_lb_logits.astype(jnp.float32), axis=0), axis=0)
    for l in range(DEPTH):
        h = x @ w_in[l]
        ua, va, qb, fb, ib, gb = jnp.split(h, SPLITS, axis=-1)
        ya = chunked_sgu(ua, va, sgu_w[l], sgu_b[l], sgu_ln_g[l], sgu_ln_b[l])
        yb = hgrn2(qb, fb, ib, gb, lb_all[l], hgrn_norm_g[l])
        mix = jnp.concatenate([ya, yb.astype(ya.dtype)], axis=-1) @ w_out[l]
        x = layer_norm(DN_ALPHA * x + mix, ln1_g[l], ln1_b[l])
        xa = memory_cross_attention(x, mem, xa_wq[l], xa_wk[l], xa_wv[l], xa_wo[l])
        x = layer_norm(DN_ALPHA * x + xa, ln2_g[l], ln2_b[l])
        ff = peer(x, peer_wq[l], peer_k1[l], peer_k2[l], peer_u[l], peer_v[l])
        x = layer_norm(DN_ALPHA * x + ff, ln3_g[l], ln3_b[l])
    return x
```

```python
import numpy as np
import concourse.bass as bass
import concourse.mybir as mybir
from concourse.bass_utils import run_bass_kernel_spmd

F32 = mybir.dt.float32
BF16 = mybir.dt.bfloat16
I32 = mybir.dt.int32
U32 = mybir.dt.uint32
AF = mybir.ActivationFunctionType
ALU = mybir.AluOpType
AX = mybir.AxisListType

D = 2048
KC = 16
NCORE = 8
TOK = 1024
NT = 2
TP = NT * 128
NPASS = TOK // TP
NPRE = 3 * TOK // TP
LN_EPS = 1e-5
ALPHA = 2.0 ** 0.25
XA_SCALE = 1.0 / (512.0 ** 0.5)
NEG = -1.0e30
SAME_ENGINE_INORDER = False


class Buf:
    __slots__ = ("w", "r", "excl")

    def __init__(self, excl=False):
        self.w = None
        self.r = []
        self.excl = excl


class Eng:
    def __init__(self, nc, name, eng, is_pe=False):
        self.eng = eng
        self.h = nc.alloc_semaphore("sem_" + name)
        self.count = 0
        self.seen = {}
        self.is_pe = is_pe


class DSem:
    ALL = []

    def __init__(self, nc, name, nobar=False):
        self.h = nc.alloc_semaphore(name)
        self.count = 0
        if not nobar:
            DSem.ALL.append(self)


class Prog:
    def __init__(self, nc):
        self.nc = nc
        self.pe = Eng(nc, "pe", nc.tensor, True)
        self.dve = Eng(nc, "dve", nc.vector)
        self.act = Eng(nc, "act", nc.scalar)
        self.pool = Eng(nc, "pool", nc.gpsimd)
        self.sp = Eng(nc, "sp", nc.sync)
        self.n_ins = 0

    def barrier(self):
        engs = (self.pe, self.dve, self.act, self.pool, self.sp)
        for E in engs:
            for F in list(engs) + DSem.ALL:
                if F is E or F.count == 0:
                    continue
                if E.seen.get(F, 0) >= F.count:
                    continue
                E.eng.wait_ge(F.h, F.count)
                E.seen[F] = F.count

    def emit(self, E, fn, reads=(), writes=(), dsem=None):
        deps = {}

        def add(tok):
            if tok is None:
                return
            s, v = tok
            if deps.get(s, 0) < v:
                deps[s] = v

        for b in reads:
            add(b.w)
            if b.excl:
                for t in b.r:
                    if t[0] is not E:
                        add(t)
        for b in writes:
            add(b.w)
            for t in b.r:
                add(t)
        for s, v in deps.items():
            if s is E and (E.is_pe or SAME_ENGINE_INORDER):
                continue
            if E.seen.get(s, 0) >= v:
                continue
            E.eng.wait_ge(s.h, v)
            E.seen[s] = v
        ins = fn(E.eng)
        self.n_ins += 1
        if dsem is None:
            E.count += 1
            ins.then_inc(E.h, 1)
            tok = (E, E.count)
        else:
            dsem.count += 16
            ins.then_inc(dsem.h, 16)
            tok = (dsem, dsem.count)
        for b in reads:
            if b.excl:
                b.r = [tok]
            else:
                b.r.append(tok)
        for b in writes:
            b.w = tok
            b.r = []
        return tok


def build_program(stage=3, stop=None, npass=NPASS, skip_pre=False, dbg=None):
    nc = bass.Bass("TRN2", target_bir_lowering=False)
    declared = []
    DSem.ALL = []
    P = Prog(nc)
    PE, DVE, ACT, POOL, SP = P.pe, P.dve, P.act, P.pool, P.sp

    def din(name, shape, dt=F32):
        declared.append(name)
        return nc.dram_tensor(name, list(shape), dt, kind="ExternalInput").ap()

    need_kv = stop != "setup"
    need_pre = stop not in ("setup", "kv")
    need_main = stop is None
    x_tok = din("x_tok", [TOK, D]) if need_main else None
    xT = din("xT", [D, TOK]) if need_main else None
    xprevT = din("xprevT", [D, 3 * TOK]) if need_pre else None
    memT = din("memT", [D, 256]) if need_kv else None
    w_sgu = din("w_sgu", [D, 8 * 256]) if need_main else None
    w_hg = din("w_hg", [D, 8 * 512]) if need_main else None
    w_fi = din("w_fi", [D, 2048]) if need_pre else None
    sgu_w = din("sgu_w", [8, 128, 128])
    sgu_bT = din("sgu_bT", [128, 8])
    sgu_ln_g = din("sgu_ln_g", [1, 1024])
    sgu_ln_b = din("sgu_ln_b", [1, 1024])
    lb_logits = din("lb_logits", [2, 1024])
    hgrn_gn = din("hgrn_gn", [1, 1024])
    w_out = din("w_out", [D, D]) if need_main else None
    xa_wq = din("xa_wq", [D, D]) if need_main and stage >= 2 else None
    xa_wk = din("xa_wk", [D, D]) if need_kv else None
    xa_wv = din("xa_wv", [D, D]) if need_kv else None
    xa_wo = din("xa_wo", [D, D]) if need_main and stage >= 2 else None
    peer_wq = din("peer_wq", [D, D]) if need_main and stage >= 3 else None
    ln_gb = [din("ln%d_%s" % (i, s), [1, D]) for i in (1, 2, 3) for s in ("g", "b")]
    k1T = din("k1T", [128, 128])
    k2T = din("k2T", [128, 128])
    peer_uv = din("peer_uv", [16384, 2 * D]) if need_main and stage >= 3 else None
    c_ident = din("c_ident", [128, 128])
    c_tri = din("c_tri", [128, 128])
    c_tri2 = din("c_tri2", [128, 128])
    c_tril = din("c_tril", [128, 128])
    c_ch = din("c_ch", [128, 2])
    c_iota16 = din("c_iota16", [128, 16])
    c_thr16 = din("c_thr16", [128, 16])
    out = nc.dram_tensor("out", [TOK, D], F32, kind="ExternalOutput").ap()

    cnt = [0]

    def sb(shape, dt=F32, name=None):
        cnt[0] += 1
        return nc.alloc_sbuf_tensor("%s_%d" % (name or "t", cnt[0]), list(shape), dt)

    class Ring:
        def __init__(self, shape, dt=F32, n=2, name="r"):
            self.t = [sb(shape, dt, name) for _ in range(n)]
            self.b = [Buf() for _ in range(n)]
            self.i = 0

        def get(self):
            k = self.i % len(self.t)
            self.i += 1
            return self.t[k], self.b[k]

    R = sb([128, NT, D], F32, "R")
    Rb = [Buf() for _ in range(NT)]
    A = sb([128, KC, TP], BF16, "A")
    Ab = [Buf() for _ in range(NT)]
    B = sb([128, KC, TP], BF16, "B")
    Bb = [Buf() for _ in range(NT)]
    NW = 2
    Wt = [sb([128, KC, 512], BF16, "W") for _ in range(NW)]
    Wb = [Buf() for _ in range(NW)]
    Wd = [DSem(nc, "dW%d" % i) for i in range(NW)]
    kT = sb([128, KC, 256], BF16, "kT")
    kTb = Buf()
    vM = sb([128, 2, D], BF16, "vM")
    vMb = Buf()
    Sf = sb([128, 8, 128], F32, "Sf")
    Sfb = [Buf() for _ in range(8)]
    Sb0 = [sb([128, 8, 128], BF16, "Sb0") for _ in range(2)]
    Sb0b = [[Buf() for _ in range(8)] for _ in range(2)]
    Sb1 = [sb([128, 8, 128], BF16, "Sb1") for _ in range(2)]
    Sb1b = [[Buf() for _ in range(8)] for _ in range(2)]
    vgb = sb([128, 1024], F32, "vgb")
    vbb = sb([128, 1024], F32, "vbb")
    lbb = sb([128, 1024], F32, "lbb")
    gnb = sb([128, 1024], F32, "gnb")
    lng = sb([128, D], F32, "lng")
    lnb = sb([128, D], F32, "lnb")
    lngb, lnbb = Buf(), Buf()
    dLN = DSem(nc, "dLN")
    WT = sb([128, 8, 128], BF16, "WT")
    bcol = sb([128, 8], F32, "bcol")
    ident = sb([128, 128], F32, "ident")
    tri = sb([128, 128], F32, "tri")
    tri2 = sb([128, 128], F32, "tri2")
    tril = sb([128, 128], F32, "tril")
    chs = sb([128, 2], F32, "chs")
    iota16 = sb([128, 1, 16], F32, "iota16")
    thr16 = sb([128, 1, 16], F32, "thr16")
    k1s = sb([128, 128], F32, "k1s")
    k2s = sb([128, 128], F32, "k2s")
    CONST = Buf()
    pidx = [sb([128, 128], I32, "pidx") for _ in range(NT)]
    pidxb = [Buf() for _ in range(NT)]
    pgates = [sb([128, 128], F32, "pgate") for _ in range(NT)]
    pgatesb = [Buf() for _ in range(NT)]
    NG = 4
    GAf = sb([128, NG * D], F32, "GA")
    Gs = [GAf[:, i * D:(i + 1) * D].bitcast(BF16) for i in range(NG)]
    Gsb = [Buf() for _ in range(NG)]
    Gsd = [DSem(nc, "dG%d" % i) for i in range(NG)]
    Gxd = [DSem(nc, "dGx%d" % i) for i in range(2)]
    dPC = DSem(nc, "dPC", nobar=True)
    uvb = nc.dram_tensor("uvb", [16384, 2 * D], BF16, kind="Internal").ap() if need_main and stage >= 3 else None
    NPC = 16
    pc_state = {"n": 0}

    def precast_chunks(k):
        if uvb is None:
            return
        rows = 16384 // NPC
        while k > 0 and pc_state["n"] < NPC:
            c = pc_state["n"]
            src = peer_uv[c * rows:(c + 1) * rows, :].rearrange("r (a b) -> (r a) b", a=2)
            dst = uvb[c * rows:(c + 1) * rows, :].rearrange("r (a b) -> (r a) b", a=2)
            P.emit(POOL, lambda e: e.dma_start(out=dst, in_=src), [], [], dsem=dPC)
            pc_state["n"] += 1
            k -= 1
    dC = DSem(nc, "dC")
    dA = DSem(nc, "dA")
    dXR = [DSem(nc, "dXR%d" % i) for i in range(NT)]
    dST = [DSem(nc, "dST%d" % i) for i in range(NT)]

    PSall = nc.alloc_psum_tensor("psall", [128, 8 * 512], F32)
    PS = [PSall[:, i * 512:(i + 1) * 512] for i in range(8)]
    PSb = [[Buf(True)] * 4 for _ in range(8)]

    def psq(bank, q, n=1):
        return PS[bank][:, q * 128:(q + n) * 128], PSb[bank][q:q + n]

    def act(out_, in_, func, reads, writes, **kw):
        return P.emit(ACT, lambda e: e.activation(out=out_, in_=in_, func=func, **kw), reads, writes)

    def tt(out_, a, b, op, reads, writes):
        return P.emit(DVE, lambda e: e.tensor_tensor(out=out_, in0=a, in1=b, op=op), reads, writes)

    def ts(out_, a, s1, s2, op0, op1, reads, writes):
        if op1 is None:
            return P.emit(DVE, lambda e: e.tensor_scalar(out=out_, in0=a, scalar1=s1, scalar2=None, op0=op0), reads, writes)
        return P.emit(DVE, lambda e: e.tensor_scalar(out=out_, in0=a, scalar1=s1, scalar2=s2, op0=op0, op1=op1), reads, writes)

    def stt(out_, a, s, b, op0, op1, reads, writes):
        return P.emit(DVE, lambda e: e.scalar_tensor_tensor(out=out_, in0=a, scalar=s, in1=b, op0=op0, op1=op1), reads, writes)

    def vcopy(out_, in_, reads, writes):
        return P.emit(DVE, lambda e: e.tensor_copy(out=out_, in_=in_), reads, writes)

    def acopy(out_, in_, reads, writes):
        return act(out_, in_, AF.Copy, reads, writes)

    def mm(out_, lhsT, rhs, start, stop, reads, writes):
        return P.emit(PE, lambda e: e.matmul(out_, lhsT=lhsT, rhs=rhs, start=start, stop=stop), reads, writes)

    def tp(out_, in_, reads, writes):
        return P.emit(PE, lambda e: e.transpose(out_, in_, ident[:]), list(reads) + [CONST], writes)

    def dma(E, out_, in_, dsem, reads, writes):
        return P.emit(E, lambda e: e.dma_start(out=out_, in_=in_), reads, writes, dsem=dsem)

    cp_flip = [0]

    def evac(out_, in_, reads, writes):
        cp_flip[0] ^= 1
        if cp_flip[0]:
            return vcopy(out_, in_, reads, writes)
        return acopy(out_, in_, reads, writes)

    def sigmoid_le(out_, in_, reads, outbuf):
        act(out_, in_, AF.Exp, reads, [outbuf], scale=-1.0)
        act(out_, out_, AF.Ln, [outbuf], [outbuf], bias=1.0)
        act(out_, out_, AF.Exp, [outbuf], [outbuf], scale=-1.0)

    def rstd_le(rs, bufs, scale, eps):
        act(rs, rs, AF.Ln, bufs, bufs, scale=scale, bias=eps)
        act(rs, rs, AF.Exp, bufs, bufs, scale=-0.5)

    def rstd_from(rs, reads_writes):
        P.emit(ACT, lambda e: e.sqrt(out=rs, in_=rs), reads_writes, reads_writes)
        P.emit(DVE, lambda e: e.reciprocal(out=rs, in_=rs), reads_writes, reads_writes)

    for t_, d_ in ((ident[:], c_ident), (tri[:], c_tri), (tri2[:], c_tri2), (tril[:], c_tril), (chs[:], c_ch),
                   (iota16[:, 0, :], c_iota16), (thr16[:, 0, :], c_thr16), (k1s[:], k1T), (k2s[:], k2T),
                   (bcol[:], sgu_bT),
                   (vgb[:], sgu_ln_g.to_broadcast([128, 1024])), (vbb[:], sgu_ln_b.to_broadcast([128, 1024])),
                   (gnb[:], hgrn_gn.to_broadcast([128, 1024])),
                   (lbb[:], lb_logits[0:1, :].to_broadcast([128, 1024])),
                   (GAf[:, 3072:4096], lb_logits[1:2, :].to_broadcast([128, 1024]))):
        dma(SP, t_, d_, dC, [], [])
    swt = GAf[:, 0:1024].rearrange("p (g s) -> p g s", g=8)
    dma(SP, swt, sgu_w.rearrange("g t s -> t g s"), dC, [], [])
    CONST.w = (dC, dC.count)
    dlt = GAf[:, 2048:3072]
    tt(dlt, lbb[:], GAf[:, 3072:4096], ALU.subtract, [CONST], [CONST])
    act(lbb[:], dlt, AF.Sigmoid, [CONST], [CONST])
    for g in range(8):
        tt(swt[:, g, :], swt[:, g, :], tril[:], ALU.mult, [CONST], [CONST])
    for g in range(8):
        pa, pb = psq(2 + (g % 2), g // 2 % 4)
        tp(pa, swt[:, g, :], [CONST], pb)
        vcopy(WT[:, g, :], pa, pb, [CONST])
    for b_ in Gsb:
        b_.w = CONST.w
        b_.r = list(CONST.r)
    P.emit(DVE, lambda e: e.memset(Sf[:], 0.0), [], Sfb)
    qeT0 = [sb([128, 128], BF16, "qeT0") for _ in range(2)]
    qeT1 = [sb([128, 128], BF16, "qeT1") for _ in range(2)]
    qeTb = [Buf(), Buf()]
    for i in range(2):
        P.emit(DVE, lambda e, i=i: e.memset(qeT0[i][:], 0.0), [], [qeTb[i]])
        P.emit(DVE, lambda e, i=i: e.memset(qeT1[i][:], 0.0), [], [qeTb[i]])

    wlist = []

    def wq_add(ap, ncols):
        wlist.append((ap, ncols))
        return len(wlist) - 1

    wstate = {"issued": 0}

    def w_issue(i):
        ap, ncols = wlist[i]
        s = i % NW
        dma(POOL, Wt[s][:, :, 0:ncols], ap.rearrange("(kc p) n -> p kc n", p=128), Wd[s], [], [Wb[s]])

    def w_acquire(i):
        while wstate["issued"] <= min(i + NW - 1, len(wlist) - 1):
            w_issue(wstate["issued"])
            wstate["issued"] += 1
            if wstate.get("trickle") and wstate["issued"] % 3 == 0:
                precast_chunks(1)
        s = i % NW
        return Wt[s], Wb[s]

    plan = {}
    if need_kv:
      plan["kv_k"] = [wq_add(xa_wk[:, n * 512:(n + 1) * 512], 512) for n in range(4)]
      plan["kv_v"] = [wq_add(xa_wv[:, n * 512:(n + 1) * 512], 512) for n in range(4)]
    if need_pre:
      plan["pre_i"] = [] if skip_pre else [wq_add(w_fi[:, n * 512:(n + 1) * 512], 512) for n in (2, 3)]
    plan["pass"] = []
    for p_ in range(npass if need_main else 0):
        d = {}
        d["sgu"] = [wq_add(w_sgu[:, g * 256:(g + 1) * 256], 256) for g in range(8)]
        d["hg"] = [wq_add(w_hg[:, h * 512:(h + 1) * 512], 512) for h in range(8)]
        d["wo1"] = [wq_add(w_out[:, n * 512:(n + 1) * 512], 512) for n in range(4)]
        if stage >= 2:
            d["xq"] = [wq_add(xa_wq[:, n * 512:(n + 1) * 512], 512) for n in range(4)]
            d["xo"] = [wq_add(xa_wo[:, n * 512:(n + 1) * 512], 512) for n in range(4)]
        if stage >= 3:
            d["pq"] = [wq_add(peer_wq[:, n * 512:(n + 1) * 512], 512) for n in range(4)]
        plan["pass"].append(d)

    def proj_tok(ps_ap, ps_bufs, Xt, Xbuf, t, Wtile, Wbuf, ncols, c0=0):
        for kc in range(KC):
            mm(ps_ap, Xt[:, kc, t * 128:(t + 1) * 128], Wtile[:, kc, c0:c0 + ncols], kc == 0, kc == KC - 1,
               [Xbuf, Wbuf], ps_bufs)

    def finish():
        for t in range(NT):
            dma(SP, out[t * 128:(t + 1) * 128, :], R[:, t, :], dST[t], [Rb[t]], [])
        for t in range(NT):
            nc.sync.wait_ge(dST[t].h, dST[t].count)
        return nc, P, declared

    if stop == "setup":
        P.emit(DVE, lambda e: e.memset(R[:], 1.0), [], Rb)
        for t in range(NT):
            vcopy(R[:, t, 0:1024], WT[:].rearrange("p a b -> p (a b)"), [CONST], [Rb[t]])
            vcopy(R[:, t, 1024:2048], lbb[:], [CONST], [Rb[t]])
        return finish()

    dma(POOL, A[:, :, 0:256], memT.rearrange("(kc p) m -> p kc m", p=128), dA, [], Ab)
    for n in range(4):
        Wtile, Wbuf = w_acquire(plan["kv_k"][n])
        for c in range(4):
            pa, pb = psq(c % 2, 0, 2)
            for kc in range(KC):
                mm(pa, Wtile[:, kc, c * 128:(c + 1) * 128], A[:, kc, 0:256], kc == 0, kc == KC - 1,
                   [Wbuf] + Ab, pb)
            evac(kT[:, n * 4 + c, :], pa, pb, [kTb])
    for n in range(4):
        Wtile, Wbuf = w_acquire(plan["kv_v"][n])
        for mc in range(2):
            pa, pb = psq(mc, 0, 4)
            for kc in range(KC):
                mm(pa, A[:, kc, mc * 128:(mc + 1) * 128], Wtile[:, kc, :], kc == 0, kc == KC - 1,
                   [Wbuf] + Ab, pb)
            evac(vM[:, mc, n * 512:(n + 1) * 512], pa, pb, [vMb])

    if stop == "kv":
        for t in range(NT):
            vcopy(R[:, t, :], vM[:, t, :], [vMb], [Rb[t]])
        vcopy(R[:, 0, 0:256], kT[:, 3, :], [kTb], [Rb[0]])
        return finish()

    r128 = {}
    ARENA_W = 10752
    arena = sb([128, ARENA_W], F32, "arena")
    arena_off = [0]

    def stage_begin():
        P.barrier()
        r128.clear()
        arena_off[0] = 0

    class ARing:
        def __init__(self, shape, dt, n):
            per = 1
            for d_ in shape[1:]:
                per *= d_
            esz = 2 if dt == BF16 else 4
            words = (per * esz + 3) // 4
            words = (words + 7) // 8 * 8
            self.t = []
            for _ in range(n):
                o = arena_off[0]
                assert o + words <= ARENA_W, ("arena overflow", o, words)
                v = arena[:, o:o + words]
                if dt != F32:
                    v = v.bitcast(dt)
                v = v[:, 0:per]
                if len(shape) == 3:
                    v = v.rearrange("p (a b) -> p a b", a=shape[1])
                self.t.append(v)
                arena_off[0] = o + words
            self.b = [Buf() for _ in range(n)]
            self.i = 0

        def get(self):
            k = self.i % len(self.t)
            self.i += 1
            return self.t[k], self.b[k]

    def T(name, dt=F32, shape=(128, 128), n=2):
        key = (name, dt, tuple(shape))
        if key not in r128:
            r128[key] = ARing(shape, dt, n)
        return r128[key].get()

    def hgrn_gate(pf, pfb, h):
        hs = slice(h * 128, (h + 1) * 128)
        sig, sigb = T("sig")
        sigmoid_le(sig[:], pf, pfb, sigb)
        f, fb = T("f")
        ts(f[:], sig[:], -1.0, 1.0, ALU.mult, ALU.add, [sigb], [fb])
        tt(f[:], f[:], lbb[:, hs], ALU.mult, [fb, CONST], [fb])
        tt(f[:], f[:], sig[:], ALU.add, [fb, sigb], [fb])
        lf, lfb = T("lf")
        act(lf[:], f[:], AF.Ln, [fb], [lfb])
        kk, kkb = T("kk")
        ts(kk[:], f[:], -1.0, 1.0, ALU.mult, ALU.add, [fb], [kkb])
        return lf, lfb, kk, kkb

    def hgrn_state(h, lf, lfb, kk, kkb, ib, ibb, bx, by, u):
        pR, pRb = psq(bx, 1)
        mm(pR, tri2[:], lf[:], True, True, [CONST, lfb], pRb)
        pL, pLb = psq(bx, 2)
        mm(pL[:, 0:2], lf[:], chs[:], True, True, [CONST, lfb], pLb)
        er, erb = T("er")
        act(er[:], pR, AF.Exp, pRb, [erb])
        eal, ealb = T("eal", F32, (128, 2))
        act(eal[:], pL[:, 0:2], AF.Exp, pLb, [ealb])
        kd0, kd0b = T("kd0", BF16)
        stt(kd0[:], kk[:], chs[:, 0:1], er[:], ALU.mult, ALU.mult, [kkb, erb, CONST], [kd0b])
        kd1, kd1b = T("kd1", BF16)
        stt(kd1[:], kk[:], chs[:, 1:2], er[:], ALU.mult, ALU.mult, [kkb, erb, CONST], [kd1b])
        acopy(Sb0[u][:, h, :], Sf[:, h, :], [Sfb[h]], [Sb0b[u][h]])
        pU0, pU0b = psq(by, 0)
        mm(pU0, kd0[:], ib[:], True, True, [kd0b, ibb], pU0b)
        stt(Sf[:, h, :], Sf[:, h, :], eal[:, 0:1], pU0, ALU.mult, ALU.add, [Sfb[h], ealb] + pU0b, [Sfb[h]])
        acopy(Sb1[u][:, h, :], Sf[:, h, :], [Sfb[h]], [Sb1b[u][h]])
        pU1, pU1b = psq(by, 1)
        mm(pU1, kd1[:], ib[:], True, True, [kd1b, ibb], pU1b)
        stt(Sf[:, h, :], Sf[:, h, :], eal[:, 1:2], pU1, ALU.mult, ALU.add, [Sfb[h], ealb] + pU1b, [Sfb[h]])

    ucount = [0]

    def hgrn_unit(h, t, pa, pb, u):
        hs = slice(h * 128, (h + 1) * 128)
        bx, by, bz = (2, 3, 4) if u == 0 else (5, 6, 7)
        lf, lfb, kk, kkb = hgrn_gate(pa[:, 128:256], pb, h)
        ib, ibb = T("ib", BF16)
        acopy(ib[:], pa[:, 256:384], pb, [ibb])
        sg, sgb = T("sg")
        sigmoid_le(sg[:], pa[:, 384:512], pb, sgb)
        tt(sg[:], pa[:, 384:512], sg[:], ALU.mult, pb + [sgb], [sgb])
        yield
        pA, pAb = psq(bx, 0)
        mm(pA, tri[:], lf[:], True, True, [CONST, lfb], pAb)
        ea, eab = T("ea")
        act(ea[:], pA, AF.Exp, pAb, [eab])
        ena, enab = T("ena")
        act(ena[:], pA, AF.Exp, pAb, [enab], scale=-1.0)
        qe, qeb = T("qe")
        tt(qe[:], pa[:, 0:128], ea[:], ALU.mult, pb + [eab], [qeb])
        ke, keb = T("ke")
        tt(ke[:], kk[:], ena[:], ALU.mult, [kkb, enab], [keb])
        yield
        hgrn_state(h, lf, lfb, kk, kkb, ib, ibb, bx, by, u)
        yield
        pT1, pT1b = psq(bz, 0)
        tp(pT1, qe[:], [qeb], pT1b)
        pT2, pT2b = psq(bz, 1)
        tp(pT2, ke[:], [keb], pT2b)
        qeT, qeTfb = T("qeT", BF16)
        acopy(qeT[:], pT1, pT1b, [qeTfb])
        vcopy(qeT0[u][:, 0:64], pT1[:, 0:64], pT1b, [qeTb[u]])
        vcopy(qeT1[u][:, 64:128], pT1[:, 64:128], pT1b, [qeTb[u]])
        keT, keTb = T("keT", BF16)
        acopy(keT[:], pT2, pT2b, [keTb])
        yield
        pS, pSb = psq(bz, 2)
        mm(pS, keT[:], qeT[:], True, True, [keTb, qeTfb], pSb)
        scT, scTb = T("scT", BF16)
        tt(scT[:], pS, tri[:], ALU.mult, pSb + [CONST], [scTb])
        yield
        pO, pOb = psq(bz, 3)
        mm(pO, scT[:], ib[:], True, False, [scTb, ibb], pOb)
        mm(pO, qeT0[u][:], Sb0[u][:, h, :], False, False, [qeTb[u], Sb0b[u][h]], pOb)
        mm(pO, qeT1[u][:], Sb1[u][:, h, :], False, True, [qeTb[u], Sb1b[u][h]], pOb)
        yield
        junk, junkb = T("junk")
        ss, ssb = T("ss", F32, (128, 1))
        act(junk[:], pO, AF.Square, pOb, [junkb, ssb], accum_out=ss[:])
        rstd_le(ss[:], [ssb], 1.0 / 128.0, LN_EPS)
        t1, t1b = T("t1")
        stt(t1[:], pO, ss[:, 0:1], gnb[:, hs], ALU.mult, ALU.mult, pOb + [ssb, CONST], [t1b])
        yb, ybb = T("yb")
        tt(yb[:], t1[:], sg[:], ALU.mult, [t1b, sgb], [ybb])
        yield
        pT3, pT3b = psq(by, 2)
        tp(pT3, yb[:], [ybb], pT3b)
        evac(B[:, 8 + h, t * 128:(t + 1) * 128], pT3, pT3b, [Bb[t]])

    def run_interleaved(starters, width=2):
        active = []
        for st_ in starters:
            active.append(st_())
            while len(active) >= width:
                oldest = active[0]
                for g_ in list(active):
                    try:
                        next(g_)
                    except StopIteration:
                        active.remove(g_)
                if oldest not in active:
                    break
        while active:
            for g_ in list(active):
                try:
                    next(g_)
                except StopIteration:
                    active.remove(g_)

    scount = [0]

    def sgu_unit(g, t, pa, pb, u):
        gs = slice(g * 128, (g + 1) * 128)
        gu, gub = T("gu")
        act(gu[:], pa[:, 0:128], AF.Gelu_apprx_tanh, pb, [gub])
        gv, gvb = T("gv")
        act(gv[:], pa[:, 128:256], AF.Gelu_apprx_tanh, pb, [gvb])
        st, stb = T("bst", F32, (128, 6))
        yield
        P.emit(DVE, lambda e: e.bn_stats(out=st[:], in_=gv[:]), [gvb], [stb])
        mv, mvb = T("bmv", F32, (128, 2))
        P.emit(DVE, lambda e: e.bn_aggr(out=mv[:], in_=st[:]), [stb], [mvb])
        rs, rsb = T("brs", F32, (128, 1))
        ts(rs[:], mv[:, 1:2], LN_EPS, None, ALU.add, None, [mvb], [rsb])
        rstd_from(rs[:], [rsb])
        yield
        vn, vnb = T("vn")
        ts(vn[:], gv[:], mv[:, 0:1], rs[:, 0:1], ALU.subtract, ALU.mult, [gvb, mvb, rsb], [vnb])
        tt(vn[:], vn[:], vgb[:, gs], ALU.mult, [vnb, CONST], [vnb])
        vb16, vb16b = T("vb16", BF16)
        tt(vb16[:], vn[:], vbb[:, gs], ALU.add, [vnb, CONST], [vb16b])
        yield
        pZ, pZb = psq(2 + 3 * u, 0)
        mm(pZ, WT[:, g, :], vb16[:], True, True, [CONST, vb16b], pZb)
        ya, yab = T("ya")
        stt(ya[:], pZ, bcol[:, g:g + 1], gu[:], ALU.add, ALU.mult, pZb + [CONST, gub], [yab])
        yield
        pT_, pTb = psq(3 + 3 * u, 0)
        tp(pT_, ya[:], [yab], pTb)
        evac(B[:, g, t * 128:(t + 1) * 128], pT_, pTb, [Bb[t]])

    NWD = 2

    WF = GAf[:, :].bitcast(BF16).rearrange("p (kc n) -> p kc n", kc=KC)
    wfb = Buf()
    wfb.w = CONST.w
    wfb.r = list(CONST.r)
    dWF = DSem(nc, "dWF")
    if need_pre and not skip_pre:
        dma(POOL, WF, w_fi[:, 0:1024].rearrange("(kc p) n -> p kc n", p=128), dWF, [], [wfb])

    def prescan_group(grp):
        for blk in range(4):
            if blk < 2:
                Wtile, Wbuf, c0 = WF, wfb, blk * 512
            else:
                ii_ = plan["pre_i"][blk - 2]
                Wtile, Wbuf, c0 = Wt[ii_ % NW], Wb[ii_ % NW], 0
            for t in range(NT):
                bank = (0 if blk < 2 else 4) + 2 * t + (blk % 2)
                pa, pb = psq(bank, 0, 4)
                proj_tok(pa, pb, A, Ab[t], t, Wtile, Wbuf, 512, c0)
        for t in range(NT):
            fb0, ib0 = 2 * t, 4 + 2 * t
            pF = PSall[:, fb0 * 512:(fb0 + 2) * 512]
            pFb = [PSb[fb0][0], PSb[fb0 + 1][0]]
            pI = PSall[:, ib0 * 512:(ib0 + 2) * 512]
            pIb = [PSb[ib0][0], PSb[ib0 + 1][0]]
            sig, sigb = T("wsig", F32, (128, 1024), NWD)
            sigmoid_le(sig, pF, pFb, sigb)
            f, fb = T("wf", F32, (128, 1024), NWD)
            ts(f, sig, -1.0, 1.0, ALU.mult, ALU.add, [sigb], [fb])
            tt(f, f, lbb[:], ALU.mult, [fb, CONST], [fb])
            tt(f, f, sig, ALU.add, [fb, sigb], [fb])
            act(sig, f, AF.Ln, [fb], [sigb])
            ts(f, f, -1.0, 1.0, ALU.mult, ALU.add, [fb], [fb])
            ib, ibb = T("wib", BF16, (128, 1024), NWD)
            acopy(ib, pI, pIb, [ibb])
            for hf in range(2):
                mm(pF[:, hf * 512:(hf + 1) * 512], tri2[:], sig[:, hf * 512:(hf + 1) * 512], True, True, [CONST, sigb], [pFb[hf]])
            er, erb = T("wer", F32, (128, 1024), NWD)
            act(er, pF, AF.Exp, pFb, [erb])
            for h in range(8):
                mm(pI[:, 2 * h:2 * h + 2], sig[:, h * 128:(h + 1) * 128], chs[:], True, True, [CONST, sigb], [pIb[0]])
            eal, ealb = T("weal", F32, (128, 8, 2), 2)
            act(eal.rearrange("p a b -> p (a b)"), pI[:, 0:16], AF.Exp, [pIb[0]], [ealb])
            kd0, kd0b = T("wkd0", BF16, (128, 1024), NWD)
            stt(kd0, f, chs[:, 0:1], er, ALU.mult, ALU.mult, [fb, erb, CONST], [kd0b])
            kd1, kd1b = T("wkd1", BF16, (128, 1024), NWD)
            stt(kd1, f, chs[:, 1:2], er, ALU.mult, ALU.mult, [fb, erb, CONST], [kd1b])
            Sf3 = Sf[:]
            for h in range(8):
                hs = slice(h * 128, (h + 1) * 128)
                mm(pF[:, hs], kd0[:, hs], ib[:, hs], True, True, [kd0b, ibb], [pFb[h // 4]])
            tt(Sf3, Sf3, eal[:, :, 0:1].to_broadcast([128, 8, 128]), ALU.mult, Sfb + [ealb], Sfb)
            tt(Sf3, Sf3, pF.rearrange("p (a b) -> p a b", a=8), ALU.add, Sfb + pFb, Sfb)
            for h in range(8):
                hs = slice(h * 128, (h + 1) * 128)
                mm(pI[:, hs], kd1[:, hs], ib[:, hs], True, True, [kd1b, ibb], [pIb[h // 4]])
            tt(Sf3, Sf3, eal[:, :, 1:2].to_broadcast([128, 8, 128]), ALU.mult, Sfb + [ealb], Sfb)
            tt(Sf3, Sf3, pI.rearrange("p (a b) -> p a b", a=8), ALU.add, Sfb + pIb, Sfb)

    if need_pre and not skip_pre:
        assert wstate["issued"] <= plan["pre_i"][-1] + 1, (wstate["issued"], plan["pre_i"])
        while wstate["issued"] <= plan["pre_i"][-1]:
            w_issue(wstate["issued"])
            wstate["issued"] += 1
    for grp in range(0 if skip_pre else NPRE):
        precast_chunks(2 if grp < 4 else 1)
        dma(POOL, A[:, :, :], xprevT[:, grp * TP:(grp + 1) * TP].rearrange("(kc p) t -> p kc t", p=128), dA, [], Ab)
        prescan_group(grp)
    for b_ in Gsb:
        if wfb.w is not None:
            b_.r.append(wfb.w)
        b_.r.extend(wfb.r)

    if stop == "prescan":
        for t in range(NT):
            vcopy(R[:, t, 0:1024], Sf[:].rearrange("p a b -> p (a b)"), Sfb, [Rb[t]])
            vcopy(R[:, t, 1024:2048], Sf[:].rearrange("p a b -> p (a b)"), Sfb, [Rb[t]])
        return finish()

    def load_ln(i):
        dma(SP, lng[:], ln_gb[2 * i].to_broadcast([128, D]), dLN, [], [lngb])
        dma(SP, lnb[:], ln_gb[2 * i + 1].to_broadcast([128, D]), dLN, [], [lnbb])
        lngb.w = (dLN, dLN.count)
        lnbb.w = (dLN, dLN.count)

    def layer_norm_tile(t):
        st, stb = T("lst", F32, (128, 4, 6))
        for c in range(4):
            P.emit(DVE, lambda e, c=c: e.bn_stats(out=st[:, c, :], in_=R[:, t, c * 512:(c + 1) * 512]), [Rb[t]], [stb])
        mv, mvb = T("lmv", F32, (128, 2))
        P.emit(DVE, lambda e: e.bn_aggr(out=mv[:], in_=st[:].rearrange("p a b -> p (a b)")), [stb], [mvb])
        rs, rsb = T("lrs", F32, (128, 1))
        act(rs[:], mv[:, 1:2], AF.Ln, [mvb], [rsb], bias=LN_EPS)
        act(rs[:], rs[:], AF.Exp, [rsb], [rsb], scale=-0.5)
        ts(R[:, t, :], R[:, t, :], mv[:, 0:1], rs[:, 0:1], ALU.subtract, ALU.mult, [Rb[t], mvb, rsb], [Rb[t]])
        tt(R[:, t, :], R[:, t, :], lng[:], ALU.mult, [Rb[t], lngb], [Rb[t]])
        tt(R[:, t, :], R[:, t, :], lnb[:], ALU.add, [Rb[t], lnbb], [Rb[t]])

    def transpose_R_to_A(t):
        for q4 in range(4):
            bank = 2 + (q4 % 2)
            pa, pb = psq(bank, 0, 4)
            for j in range(4):
                kc = q4 * 4 + j
                tp(pa[:, j * 128:(j + 1) * 128], R[:, t, kc * 128:(kc + 1) * 128], [Rb[t]], [pb[j]])
            evac(A[:, q4 * 4:(q4 + 1) * 4, t * 128:(t + 1) * 128], pa.rearrange("p (a b) -> p a b", a=4), pb, [Ab[t]])

    def out_proj_residual(wids, first):
        for n in range(4):
            Wtile, Wbuf = w_acquire(wids[n])
            for t in range(NT):
                pa, pb = psq((n * NT + t) % 2, 0, 4)
                proj_tok(pa, pb, B, Bb[t], t, Wtile, Wbuf, 512)
                stt(R[:, t, n * 512:(n + 1) * 512], R[:, t, n * 512:(n + 1) * 512], ALPHA, pa, ALU.mult, ALU.add,
                    [Rb[t]] + pb, [Rb[t]])

    for ps_ in range(npass):
        pl = plan["pass"][ps_]
        t0 = ps_ * TP
        stage_begin()
        dma(POOL, A[:, :, :], xT[:, t0:t0 + TP].rearrange("(kc p) t -> p kc t", p=128), dA, [], Ab)
        for t in range(NT):
            dma(SP, R[:, t, :], x_tok[t0 + t * 128:t0 + (t + 1) * 128, :], dXR[t], [], [Rb[t]])
        load_ln(0)
        if ps_ == 0:
            wstate["trickle"] = True
        if dbg == "p_load":
            return finish()
        ucnt = [0]

        def sgu_start(g, t):
            def go():
                Wtile, Wbuf = w_acquire(pl["sgu"][g])
                u = ucnt[0] % 2
                ucnt[0] += 1
                pa, pb = psq(u, 0, 2)
                proj_tok(pa, pb, A, Ab[t], t, Wtile, Wbuf, 256)
                return sgu_unit(g, t, pa, pb, u)
            return go

        run_interleaved([sgu_start(g, t) for g in range(8) for t in range(NT)])
        if dbg == "p_sgu":
            return finish()

        def hg_start(h, t):
            def go():
                Wtile, Wbuf = w_acquire(pl["hg"][h])
                u = ucnt[0] % 2
                ucnt[0] += 1
                pa, pb = psq(u, 0, 4)
                proj_tok(pa, pb, A, Ab[t], t, Wtile, Wbuf, 512)
                return hgrn_unit(h, t, pa, pb, u)
            return go

        run_interleaved([hg_start(h, t) for h in range(8) for t in range(NT)])
        if dbg in ("p_hg", "hg1", "hg2", "hg3"):
            return finish()
        out_proj_residual(pl["wo1"], True)
        if dbg == "p_wo":
            return finish()
        for t in range(NT):
            layer_norm_tile(t)
            if stage >= 2:
                transpose_R_to_A(t)
        if stage >= 2:
            stage_begin()
            load_ln(1)
            for h in range(4):
                Wtile, Wbuf = w_acquire(pl["xq"][h])
                qT, qTb = T("qT", BF16, (128, 4, TP), 2)
                for c in range(4):
                    pa, pb = psq(c % 2, 0, NT)
                    for kc in range(KC):
                        mm(pa, Wtile[:, kc, c * 128:(c + 1) * 128], A[:, kc, :], kc == 0, kc == KC - 1,
                           [Wbuf] + Ab, pb)
                    evac(qT[:, c, :], pa, pb, [qTb])
                for t in range(NT):
                    pS, pSb = psq(4, 2 * (t % 2), 2)
                    for c in range(4):
                        mm(pS, qT[:, c, t * 128:(t + 1) * 128], kT[:, h * 4 + c, :], c == 0, c == 3, [qTb, kTb], pSb)
                    mx, mxb = T("mx", F32, (128, 1))
                    P.emit(DVE, lambda e: e.tensor_reduce(out=mx[:], in_=pS, axis=AX.X, op=ALU.max), pSb, [mxb])
                    ts(mx[:], mx[:], -XA_SCALE, None, ALU.mult, None, [mxb], [mxb])
                    pe_, peb = T("pexp", F32, (128, 256))
                    sm, smb = T("sm", F32, (128, 1))
                    act(pe_[:], pS, AF.Exp, pSb + [mxb], [peb, smb], bias=mx[:, 0:1], scale=XA_SCALE, accum_out=sm[:])
                    P.emit(DVE, lambda e: e.reciprocal(out=sm[:], in_=sm[:]), [smb], [smb])
                    pTT, pTTb = psq(5, 2 * (t % 2), 2)
                    for mc in range(2):
                        tp(pTT[:, mc * 128:(mc + 1) * 128], pe_[:, mc * 128:(mc + 1) * 128], [peb], [pTTb[mc]])
                    pT16, pT16b = T("pT16", BF16, (128, 2, 128))
                    evac(pT16[:], pTT.rearrange("p (a b) -> p a b", a=2), pTTb, [pT16b])
                    pO, pOb = psq(6 + (t % 2), 0, 4)
                    for mc in range(2):
                        mm(pO, pT16[:, mc, :], vM[:, mc, h * 512:(h + 1) * 512], mc == 0, mc == 1, [pT16b, vMb], pOb)
                    oh, ohb = T("oh", F32, (128, 512))
                    ts(oh[:], pO, sm[:, 0:1], None, ALU.mult, None, pOb + [smb], [ohb])
                    pTo, pTob = psq(2 + (t % 2), 0, 4)
                    for j in range(4):
                        tp(pTo[:, j * 128:(j + 1) * 128], oh[:, j * 128:(j + 1) * 128], [ohb], [pTob[j]])
                    evac(B[:, h * 4:(h + 1) * 4, t * 128:(t + 1) * 128], pTo.rearrange("p (a b) -> p a b", a=4), pTob, [Bb[t]])
            out_proj_residual(pl["xo"], False)
            for t in range(NT):
                layer_norm_tile(t)
                if stage >= 3:
                    transpose_R_to_A(t)
        if stage >= 3:
            stage_begin()
            load_ln(2)
            if ps_ == 0:
                precast_chunks(NPC)
                for b_ in Gsb:
                    b_.r.append((dPC, dPC.count))
            peer_stage(nc, P, locals())
        for t in range(NT):
            dma(SP, out[t0 + t * 128:t0 + (t + 1) * 128, :], R[:, t, :], dST[t], [Rb[t]], [])

    for t in range(NT):
        nc.sync.wait_ge(dST[t].h, dST[t].count)
    return nc, P, declared


def peer_stage(nc, P, L):
    PE, DVE, ACT, POOL, SP = P.pe, P.dve, P.act, P.pool, P.sp
    T = L["T"]; psq = L["psq"]; mm = L["mm"]; act = L["act"]; tt = L["tt"]; ts = L["ts"]; stt = L["stt"]
    vcopy = L["vcopy"]; acopy = L["acopy"]; evac = L["evac"]; w_acquire = L["w_acquire"]
    A = L["A"]; Ab = L["Ab"]; R = L["R"]; Rb = L["Rb"]; CONST = L["CONST"]
    k1s = L["k1s"]; k2s = L["k2s"]; iota16 = L["iota16"]; thr16 = L["thr16"]; ident = L["ident"]
    pl = L["pl"]; layer_norm_tile = L["layer_norm_tile"]
    sb = L["sb"]

    idx, idxb, gates, gatesb = L["pidx"], L["pidxb"], L["pgates"], L["pgatesb"]

    def route(t, h, pS, pSb):
        s12, s12b = T("s12", F32, (128, 256))
        acopy(s12[:], pS, pSb, [s12b])
        v1, v1b = T("v1", F32, (128, 16, 1))
        v2, v2b = T("v2", F32, (128, 1, 16))
        i12u, i12ub = T("i12u", U32, (128, 32))
        s12m, s12mb = T("s12m", F32, (128, 256))
        for half, vt, vb_ in ((0, v1[:, :, 0], v1b), (1, v2[:, 0, :], v2b)):
            sl = slice(half * 128, (half + 1) * 128)
            P.emit(DVE, lambda e: e.max(out=vt[:, 0:8], in_=s12[:, sl]), [s12b], [vb_])
            P.emit(DVE, lambda e: e.match_replace(out=s12m[:, sl], in_to_replace=vt[:, 0:8], in_values=s12[:, sl], imm_value=NEG),
                   [s12b, vb_], [s12mb])
            P.emit(DVE, lambda e: e.max(out=vt[:, 8:16], in_=s12m[:, sl]), [s12mb, vb_], [vb_])
            P.emit(DVE, lambda e: e.max_index(out=i12u[:, half * 16:half * 16 + 8], in_max=vt[:, 0:8], in_values=s12[:, sl]),
                   [s12b, vb_], [i12ub])
            P.emit(DVE, lambda e: e.max_index(out=i12u[:, half * 16 + 8:half * 16 + 16], in_max=vt[:, 8:16], in_values=s12[:, sl]),
                   [s12b, vb_], [i12ub])
        i1f, i1fb = T("i1f", F32, (128, 1, 16))
        i2f, i2fb = T("i2f", F32, (128, 1, 16))
        vcopy(i1f[:, 0, :], i12u[:, 0:16], [i12ub], [i1fb])
        vcopy(i2f[:, 0, :], i12u[:, 16:32], [i12ub], [i2fb])
        cand, candb = T("cand", F32, (128, 256))
        tt(cand[:].rearrange("p (a b) -> p a b", a=16), v1[:].to_broadcast([128, 16, 16]), v2[:].to_broadcast([128, 16, 16]),
           ALU.add, [v1b, v2b], [candb])
        tsv, tsb = T("tsv", F32, (128, 16))
        candm, candmb = T("candm", F32, (128, 256))
        posu, posub = T("posu", U32, (128, 16))
        P.emit(DVE, lambda e: e.max(out=tsv[:, 0:8], in_=cand[:]), [candb], [tsb])
        P.emit(DVE, lambda e: e.match_replace(out=candm[:], in_to_replace=tsv[:, 0:8], in_values=cand[:], imm_value=NEG),
               [candb, tsb], [candmb])
        P.emit(DVE, lambda e: e.max(out=tsv[:, 8:16], in_=candm[:]), [candmb, tsb], [tsb])
        P.emit(DVE, lambda e: e.max_index(out=posu[:, 0:8], in_max=tsv[:, 0:8], in_values=cand[:]), [candb, tsb], [posub])
        P.emit(DVE, lambda e: e.max_index(out=posu[:, 8:16], in_max=tsv[:, 8:16], in_values=cand[:]), [candb, tsb], [posub])
        nb, nbb = T("nb", F32, (128, 1))
        ts(nb[:], tsv[:, 0:1], -1.0, None, ALU.mult, None, [tsb], [nbb])
        ex, exb = T("ex", F32, (128, 16))
        sm, smb = T("psm", F32, (128, 1))
        act(ex[:], tsv[:], AF.Exp, [tsb, nbb], [exb, smb], bias=nb[:, 0:1], scale=1.0, accum_out=sm[:])
        P.emit(DVE, lambda e: e.reciprocal(out=sm[:], in_=sm[:]), [smb], [smb])
        ts(gates[t][:, h * 16:(h + 1) * 16], ex[:], sm[:, 0:1], None, ALU.mult, None, [exb, smb], [gatesb[t]])
        posf, posfb = T("posf", F32, (128, 16, 1))
        vcopy(posf[:, :, 0], posu[:], [posub], [posfb])
        oh, ohb = T("ohot", F32, (128, 16, 16))
        tt(oh[:], posf[:].to_broadcast([128, 16, 16]), thr16[:].to_broadcast([128, 16, 16]), ALU.is_ge, [posfb, CONST], [ohb])
        af, afb = T("af", F32, (128, 16, 1))
        P.emit(DVE, lambda e: e.tensor_reduce(out=af[:, :, 0], in_=oh[:], axis=AX.X, op=ALU.add), [ohb], [afb])
        bf_, bfb = T("bf", F32, (128, 16, 1))
        stt(bf_[:, :, 0], af[:, :, 0], -16.0, posf[:, :, 0], ALU.mult, ALU.add, [afb, posfb], [bfb])
        e12 = []
        for sel, selb, srcf, srcfb in ((af, afb, i1f, i1fb), (bf_, bfb, i2f, i2fb)):
            oh2, oh2b = T("ohot", F32, (128, 16, 16))
            tt(oh2[:], sel[:].to_broadcast([128, 16, 16]), iota16[:].to_broadcast([128, 16, 16]), ALU.is_equal,
               [selb, CONST], [oh2b])
            tt(oh2[:], oh2[:], srcf[:].to_broadcast([128, 16, 16]), ALU.mult, [oh2b, srcfb], [oh2b])
            ee, eeb = T("ee", F32, (128, 16))
            P.emit(DVE, lambda e, ee=ee, oh2=oh2: e.tensor_reduce(out=ee[:], in_=oh2[:], axis=AX.X, op=ALU.add), [oh2b], [eeb])
            e12.append((ee, eeb))
        stt(idx[t][:, h * 16:(h + 1) * 16], e12[0][0][:], 128.0, e12[1][0][:], ALU.mult, ALU.add,
            [e12[0][1], e12[1][1]], [idxb[t]])

    for n in range(4):
        Wtile, Wbuf = w_acquire(pl["pq"][n])
        for hh in range(2):
            h = n * 2 + hh
            pSs = [psq(2 + t % 2, 2 * ((t // 2) % 2), 2) for t in range(NT)]
            for part in range(2):
                c = hh * 2 + part
                pa, pb = psq(c % 2, 0, NT)
                for kc in range(KC):
                    mm(pa, Wtile[:, kc, c * 128:(c + 1) * 128], A[:, kc, :], kc == 0, kc == KC - 1, [Wbuf] + Ab, pb)
                qc, qcb = T("qc", F32, (128, TP))
                evac(qc[:], pa, pb, [qcb])
                for t in range(NT):
                    pS, pSb = pSs[t]
                    mm(pS[:, part * 128:(part + 1) * 128], qc[:, t * 128:(t + 1) * 128], (k1s if part == 0 else k2s)[:],
                       True, True, [qcb, CONST], [pSb[part]])
            for t in range(NT):
                route(t, h, pSs[t][0], pSs[t][1])

    uvb = L["uvb"]
    Wt1 = L["Wt"][1][:].rearrange("p a b -> p (a b)")
    Wb1 = L["Wb"][1]
    assert pl["pq"][3] % 2 == 1, "last weight block of the pass must live in slot 1"
    xbufs = [Buf(), Buf()]
    for xb_ in xbufs:
        xb_.w = Wb1.w
        xb_.r = list(Wb1.r)
    Gs = list(L["Gs"]) + [Wt1[:, 0:2 * D], Wt1[:, 2 * D:4 * D]]
    Gsb = list(L["Gsb"]) + xbufs
    Gsd = list(L["Gsd"]) + list(L["Gxd"])
    NG = len(Gs)
    for t in range(NT):
        ffq = [psq(4 + q, 0, 4) for q in range(4)]
        pend = None
        LOOK = NG - 1

        def gather(jj):
            sg_ = jj % NG
            P.emit(POOL, lambda e: e.indirect_dma_start(out=Gs[sg_], out_offset=None, in_=uvb,
                                                       in_offset=bass.IndirectOffsetOnAxis(ap=idx[t][:, jj:jj + 1], axis=0)),
                   [idxb[t]], [Gsb[sg_]], dsem=Gsd[sg_])

        for jj in range(LOOK):
            gather(jj)
        for j in range(128 + 1):
            if pend is not None:
                pj, pvs, pgh, pghb = pend
                dg, dgb = T("dg", BF16, (128, 128), 3)
                ts(dg[:], ident[:], pgh[:, 0:1], gates[t][:, pj:pj + 1], ALU.mult, ALU.mult, [CONST, pghb, gatesb[t]], [dgb])
                for q in range(4):
                    mm(ffq[q][0], dg[:], Gs[pvs][:, D + q * 512:D + (q + 1) * 512], pj == 0, pj == 127, [dgb, Gsb[pvs]], ffq[q][1])
                pend = None
            if j < 128:
                if j + LOOK < 128:
                    gather(j + LOOK)
                s = j % NG
                hd, hdb = T("hdj", F32, (128, 1), 4)
                junk, junkb = T("pjunk", BF16, (128, D), 2)
                P.emit(DVE, lambda e: e.scalar_tensor_tensor(out=junk, in0=R[:, t, :], scalar=1.0, in1=Gs[s][:, 0:D],
                                                             op0=ALU.mult, op1=ALU.mult, accum_out=hd[:]),
                       [Rb[t], Gsb[s]], [junkb, hdb])
                gh, ghb = T("ghj", F32, (128, 1), 4)
                act(gh[:], hd[:], AF.Gelu_apprx_tanh, [hdb], [ghb])
                pend = (j, s, gh, ghb)
        for q in range(4):
            stt(R[:, t, q * 512:(q + 1) * 512], R[:, t, q * 512:(q + 1) * 512], ALPHA, ffq[q][0], ALU.mult, ALU.add,
                [Rb[t]] + ffq[q][1], [Rb[t]])
        layer_norm_tile(t)
    for xb_ in xbufs:
        if xb_.w is not None:
            Wb1.r.append(xb_.w)
        Wb1.r.extend(xb_.r)


def _consts():
    s = np.arange(128)
    same = (s[:, None] // 64) == (s[None, :] // 64)
    c = {}
    c["c_ident"] = np.eye(128, dtype=np.float32)
    c["c_tri"] = ((s[:, None] <= s[None, :]) & same).astype(np.float32)
    c["c_tri2"] = ((s[:, None] > s[None, :]) & same).astype(np.float32)
    c["c_tril"] = (s[:, None] >= s[None, :]).astype(np.float32)
    c["c_ch"] = np.stack([(s < 64), (s >= 64)], axis=1).astype(np.float32)
    c["c_iota16"] = np.tile(np.arange(16, dtype=np.float32)[None, :], (128, 1))
    c["c_thr16"] = np.tile((16.0 * (np.arange(16, dtype=np.float32) + 1.0))[None, :], (128, 1))
    return c


def make_in_maps(inputs):
    f = lambda a: np.ascontiguousarray(np.asarray(a, dtype=np.float32))
    x = f(inputs["x"]); mem = f(inputs["mem"])
    w_in = f(inputs["w_in"])[0]
    MA = 1024
    ua = w_in[:, 0:MA].reshape(D, 8, 128); va = w_in[:, MA:2 * MA].reshape(D, 8, 128)
    w_sgu = np.ascontiguousarray(np.concatenate([ua, va], axis=2).reshape(D, 8 * 256))
    qb, fb, ib, gb = [w_in[:, 2 * MA + i * 1024:2 * MA + (i + 1) * 1024].reshape(D, 8, 128) for i in range(4)]
    w_hg = np.ascontiguousarray(np.concatenate([qb, fb, ib, gb], axis=2).reshape(D, 8 * 512))
    shared = {
        "w_sgu": w_sgu, "w_hg": w_hg, "w_fi": np.ascontiguousarray(w_in[:, 3 * MA:5 * MA]),
        "sgu_w": f(inputs["sgu_w"])[0], "sgu_bT": np.ascontiguousarray(f(inputs["sgu_b"])[0].T),
        "sgu_ln_g": f(inputs["sgu_ln_g"])[0].reshape(1, 1024), "sgu_ln_b": f(inputs["sgu_ln_b"])[0].reshape(1, 1024),
        "lb_logits": f(inputs["hgrn_lb_logits"]), "hgrn_gn": f(inputs["hgrn_norm_g"])[0].reshape(1, 1024),
        "w_out": f(inputs["w_out"])[0], "xa_wq": f(inputs["xa_wq"])[0], "xa_wk": f(inputs["xa_wk"])[0],
        "xa_wv": f(inputs["xa_wv"])[0], "xa_wo": f(inputs["xa_wo"])[0], "peer_wq": f(inputs["peer_wq"])[0],
        "ln1_g": f(inputs["ln1_g"]), "ln1_b": f(inputs["ln1_b"]), "ln2_g": f(inputs["ln2_g"]), "ln2_b": f(inputs["ln2_b"]),
        "ln3_g": f(inputs["ln3_g"]), "ln3_b": f(inputs["ln3_b"]),
        "k1T": np.ascontiguousarray(f(inputs["peer_k1"])[0].T), "k2T": np.ascontiguousarray(f(inputs["peer_k2"])[0].T),
        "peer_uv": np.ascontiguousarray(np.concatenate([f(inputs["peer_u"])[0], f(inputs["peer_v"])[0]], axis=1)),
    }
    shared.update(_consts())
    maps = []
    for c in range(NCORE):
        b, j = c // 4, c % 4
        xs = x[b, j * TOK:(j + 1) * TOK]
        prev = np.zeros((3 * TOK, D), np.float32)
        if j > 0:
            prev[(3 - j) * TOK:] = x[b, 0:j * TOK]
        m = dict(shared)
        m["x_tok"] = np.ascontiguousarray(xs)
        m["xT"] = np.ascontiguousarray(xs.T)
        m["xprevT"] = np.ascontiguousarray(prev.T)
        m["memT"] = np.ascontiguousarray(mem[b].T)
        maps.append(m)
    return maps


def kernel(**inputs):
    nc, _, declared = build_program(3)
    maps = [{k: m[k] for k in declared} for m in make_in_maps(inputs)]
    res = run_bass_kernel_spmd(nc, maps, core_ids=list(range(NCORE)))
    outs = [np.asarray(r["out"]).reshape(TOK, D) for r in res.results]
    full = np.stack([np.concatenate(outs[0:4], axis=0), np.concatenate(outs[4:8], axis=0)], axis=0)
    return full.astype(np.float32)
```

```python
import numpy as np
import concourse.bass as bass
import concourse.mybir as mybir
from concourse.bass_utils import run_bass_kernel_spmd

F32 = mybir.dt.float32
BF16 = mybir.dt.bfloat16
I32 = mybir.dt.int32
U32 = mybir.dt.uint32
AF = mybir.ActivationFunctionType
ALU = mybir.AluOpType
AX = mybir.AxisListType

D = 2048
KC = 16
NCORE = 8
TOK = 1024
NT = 2
TP = NT * 128
NPASS = TOK // TP
NPRE = 3 * TOK // TP
LN_EPS = 1e-5
ALPHA = 2.0 ** 0.25
XA_SCALE = 1.0 / (512.0 ** 0.5)
NEG = -1.0e30
SAME_ENGINE_INORDER = False


class Buf:
    __slots__ = ("w", "r", "excl")

    def __init__(self, excl=False):
        self.w = None
        self.r = []
        self.excl = excl


class Eng:
    def __init__(self, nc, name, eng, is_pe=False):
        self.eng = eng
        self.h = nc.alloc_semaphore("sem_" + name)
        self.count = 0
        self.seen = {}
        self.is_pe = is_pe


class DSem:
    ALL = []

    def __init__(self, nc, name, nobar=False):
        self.h = nc.alloc_semaphore(name)
        self.count = 0
        if not nobar:
            DSem.ALL.append(self)


class Prog:
    def __init__(self, nc):
        self.nc = nc
        self.pe = Eng(nc, "pe", nc.tensor, True)
        self.dve = Eng(nc, "dve", nc.vector)
        self.act = Eng(nc, "act", nc.scalar)
        self.pool = Eng(nc, "pool", nc.gpsimd)
        self.sp = Eng(nc, "sp", nc.sync)
        self.n_ins = 0

    def barrier(self):
        engs = (self.pe, self.dve, self.act, self.pool, self.sp)
        for E in engs:
            for F in list(engs) + DSem.ALL:
                if F is E or F.count == 0:
                    continue
                if E.seen.get(F, 0) >= F.count:
                    continue
                E.eng.wait_ge(F.h, F.count)
                E.seen[F] = F.count

    def emit(self, E, fn, reads=(), writes=(), dsem=None):
        deps = {}

        def add(tok):
            if tok is None:
                return
            s, v = tok
            if deps.get(s, 0) < v:
                deps[s] = v

        for b in reads:
            add(b.w)
            if b.excl:
                for t in b.r:
                    if t[0] is not E:
                        add(t)
        for b in writes:
            add(b.w)
            for t in b.r:
                add(t)
        for s, v in deps.items():
            if s is E and (E.is_pe or SAME_ENGINE_INORDER):
                continue
            if E.seen.get(s, 0) >= v:
                continue
            E.eng.wait_ge(s.h, v)
            E.seen[s] = v
        ins = fn(E.eng)
        self.n_ins += 1
        if dsem is None:
            E.count += 1
            ins.then_inc(E.h, 1)
            tok = (E, E.count)
        else:
            dsem.count += 16
            ins.then_inc(dsem.h, 16)
            tok = (dsem, dsem.count)
        for b in reads:
            if b.excl:
                b.r = [tok]
            else:
                b.r.append(tok)
        for b in writes:
            b.w = tok
            b.r = []
        return tok


def build_program(stage=3, stop=None, npass=NPASS, skip_pre=False, dbg=None):
    nc = bass.Bass("TRN2", target_bir_lowering=False)
    declared = []
    DSem.ALL = []
    P = Prog(nc)
    PE, DVE, ACT, POOL, SP = P.pe, P.dve, P.act, P.pool, P.sp

    def din(name, shape, dt=F32):
        declared.append(name)
        return nc.dram_tensor(name, list(shape), dt, kind="ExternalInput").ap()

    need_kv = stop != "setup"
    need_pre = stop not in ("setup", "kv")
    need_main = stop is None
    x_tok = din("x_tok", [TOK, D]) if need_main else None
    xT = din("xT", [D, TOK]) if need_main else None
    xprevT = din("xprevT", [D, 3 * TOK]) if need_pre else None
    memT = din("memT", [D, 256]) if need_kv else None
    w_sgu = din("w_sgu", [D, 8 * 256]) if need_main else None
    w_hg = din("w_hg", [D, 8 * 512]) if need_main else None
    w_fi = din("w_fi", [D, 2048]) if need_pre else None
    sgu_w = din("sgu_w", [8, 128, 128])
    sgu_bT = din("sgu_bT", [128, 8])
    sgu_ln_g = din("sgu_ln_g", [1, 1024])
    sgu_ln_b = din("sgu_ln_b", [1, 1024])
    lb_logits = din("lb_logits", [2, 1024])
    hgrn_gn = din("hgrn_gn", [1, 1024])
    w_out = din("w_out", [D, D]) if need_main else None
    xa_wq = din("xa_wq", [D, D]) if need_main and stage >= 2 else None
    xa_wk = din("xa_wk", [D, D]) if need_kv else None
    xa_wv = din("xa_wv", [D, D]) if need_kv else None
    xa_wo = din("xa_wo", [D, D]) if need_main and stage >= 2 else None
    peer_wq = din("peer_wq", [D, D]) if need_main and stage >= 3 else None
    ln_gb = [din("ln%d_%s" % (i, s), [1, D]) for i in (1, 2, 3) for s in ("g", "b")]
    k1T = din("k1T", [128, 128])
    k2T = din("k2T", [128, 128])
    peer_uv = din("peer_uv", [16384, 2 * D]) if need_main and stage >= 3 else None
    c_ident = din("c_ident", [128, 128])
    c_tri = din("c_tri", [128, 128])
    c_tri2 = din("c_tri2", [128, 128])
    c_tril = din("c_tril", [128, 128])
    c_ch = din("c_ch", [128, 2])
    c_iota16 = din("c_iota16", [128, 16])
    c_thr16 = din("c_thr16", [128, 16])
    out = nc.dram_tensor("out", [TOK, D], F32, kind="ExternalOutput").ap()

    cnt = [0]

    def sb(shape, dt=F32, name=None):
        cnt[0] += 1
        return nc.alloc_sbuf_tensor("%s_%d" % (name or "t", cnt[0]), list(shape), dt)

    class Ring:
        def __init__(self, shape, dt=F32, n=2, name="r"):
            self.t = [sb(shape, dt, name) for _ in range(n)]
            self.b = [Buf() for _ in range(n)]
            self.i = 0

        def get(self):
            k = self.i % len(self.t)
            self.i += 1
            return self.t[k], self.b[k]

    R = sb([128, NT, D], F32, "R")
    Rb = [Buf() for _ in range(NT)]
    A = sb([128, KC, TP], BF16, "A")
    Ab = [Buf() for _ in range(NT)]
    B = sb([128, KC, TP], BF16, "B")
    Bb = [Buf() for _ in range(NT)]
    NW = 2
    Wt = [sb([128, KC, 512], BF16, "W") for _ in range(NW)]
    Wb = [Buf() for _ in range(NW)]
    Wd = [DSem(nc, "dW%d" % i) for i in range(NW)]
    kT = sb([128, KC, 256], BF16, "kT")
    kTb = Buf()
    vM = sb([128, 2, D], BF16, "vM")
    vMb = Buf()
    Sf = sb([128, 8, 128], F32, "Sf")
    Sfb = [Buf() for _ in range(8)]
    Sb0 = [sb([128, 8, 128], BF16, "Sb0") for _ in range(2)]
    Sb0b = [[Buf() for _ in range(8)] for _ in range(2)]
    Sb1 = [sb([128, 8, 128], BF16, "Sb1") for _ in range(2)]
    Sb1b = [[Buf() for _ in range(8)] for _ in range(2)]
    vgb = sb([128, 1024], F32, "vgb")
    vbb = sb([128, 1024], F32, "vbb")
    lbb = sb([128, 1024], F32, "lbb")
    gnb = sb([128, 1024], F32, "gnb")
    lng = sb([128, D], F32, "lng")
    lnb = sb([128, D], F32, "lnb")
    lngb, lnbb = Buf(), Buf()
    dLN = DSem(nc, "dLN")
    WT = sb([128, 8, 128], BF16, "WT")
    bcol = sb([128, 8], F32, "bcol")
    ident = sb([128, 128], F32, "ident")
    tri = sb([128, 128], F32, "tri")
    tri2 = sb([128, 128], F32, "tri2")
    tril = sb([128, 128], F32, "tril")
    chs = sb([128, 2], F32, "chs")
    iota16 = sb([128, 1, 16], F32, "iota16")
    thr16 = sb([128, 1, 16], F32, "thr16")
    k1s = sb([128, 128], F32, "k1s")
    k2s = sb([128, 128], F32, "k2s")
    mhalf = sb([128, 1], F32, "mhalf")
    CONST = Buf()
    pidx = [sb([128, 128], I32, "pidx") for _ in range(NT)]
    pidxb = [Buf() for _ in range(NT)]
    pgates = [sb([128, 128], F32, "pgate") for _ in range(NT)]
    pgatesb = [Buf() for _ in range(NT)]
    NG = 4
    GAf = sb([128, NG * D], F32, "GA")
    Gs = [GAf[:, i * D:(i + 1) * D].bitcast(BF16) for i in range(NG)]
    Gsb = [Buf() for _ in range(NG)]
    Gsd = [DSem(nc, "dG%d" % i) for i in range(NG)]
    Gxd = [DSem(nc, "dGx%d" % i) for i in range(2)]
    dPC = DSem(nc, "dPC", nobar=True)
    uvb = nc.dram_tensor("uvb", [16384, 2 * D], BF16, kind="Internal").ap() if need_main and stage >= 3 else None
    NPC = 16
    pc_state = {"n": 0}

    def precast_chunks(k):
        if uvb is None:
            return
        rows = 16384 // NPC
        while k > 0 and pc_state["n"] < NPC:
            c = pc_state["n"]
            src = peer_uv[c * rows:(c + 1) * rows, :].rearrange("r (a b) -> (r a) b", a=2)
            dst = uvb[c * rows:(c + 1) * rows, :].rearrange("r (a b) -> (r a) b", a=2)
            P.emit(POOL, lambda e: e.dma_start(out=dst, in_=src), [], [], dsem=dPC)
            pc_state["n"] += 1
            k -= 1
    dC = DSem(nc, "dC")
    dA = DSem(nc, "dA")
    dXR = [DSem(nc, "dXR%d" % i) for i in range(NT)]
    dST = [DSem(nc, "dST%d" % i) for i in range(NT)]

    PSall = nc.alloc_psum_tensor("psall", [128, 8 * 512], F32)
    PS = [PSall[:, i * 512:(i + 1) * 512] for i in range(8)]
    PSb = [[Buf(True)] * 4 for _ in range(8)]

    def psq(bank, q, n=1):
        return PS[bank][:, q * 128:(q + n) * 128], PSb[bank][q:q + n]

    def act(out_, in_, func, reads, writes, **kw):
        return P.emit(ACT, lambda e: e.activation(out=out_, in_=in_, func=func, **kw), reads, writes)

    def tt(out_, a, b, op, reads, writes):
        return P.emit(DVE, lambda e: e.tensor_tensor(out=out_, in0=a, in1=b, op=op), reads, writes)

    def ts(out_, a, s1, s2, op0, op1, reads, writes):
        if op1 is None:
            return P.emit(DVE, lambda e: e.tensor_scalar(out=out_, in0=a, scalar1=s1, scalar2=None, op0=op0), reads, writes)
        return P.emit(DVE, lambda e: e.tensor_scalar(out=out_, in0=a, scalar1=s1, scalar2=s2, op0=op0, op1=op1), reads, writes)

    def stt(out_, a, s, b, op0, op1, reads, writes):
        return P.emit(DVE, lambda e: e.scalar_tensor_tensor(out=out_, in0=a, scalar=s, in1=b, op0=op0, op1=op1), reads, writes)

    def vcopy(out_, in_, reads, writes):
        return P.emit(DVE, lambda e: e.tensor_copy(out=out_, in_=in_), reads, writes)

    def acopy(out_, in_, reads, writes):
        return act(out_, in_, AF.Copy, reads, writes)

    def mm(out_, lhsT, rhs, start, stop, reads, writes):
        return P.emit(PE, lambda e: e.matmul(out_, lhsT=lhsT, rhs=rhs, start=start, stop=stop), reads, writes)

    def tp(out_, in_, reads, writes):
        return P.emit(PE, lambda e: e.transpose(out_, in_, ident[:]), list(reads) + [CONST], writes)

    def dma(E, out_, in_, dsem, reads, writes):
        return P.emit(E, lambda e: e.dma_start(out=out_, in_=in_), reads, writes, dsem=dsem)

    cp_flip = [0]

    def evac(out_, in_, reads, writes):
        cp_flip[0] ^= 1
        if cp_flip[0]:
            return vcopy(out_, in_, reads, writes)
        return acopy(out_, in_, reads, writes)

    def sigmoid_le(out_, in_, reads, outbuf):
        act(out_, in_, AF.Exp, reads, [outbuf], scale=-1.0)
        act(out_, out_, AF.Ln, [outbuf], [outbuf], bias=1.0)
        act(out_, out_, AF.Exp, [outbuf], [outbuf], scale=-1.0)

    def rstd_le(rs, bufs, scale, eps):
        act(rs, rs, AF.Ln, bufs, bufs, scale=scale, bias=eps)
        act(rs, rs, AF.Exp, bufs, bufs, scale=-0.5)

    def rstd_from(rs, reads_writes):
        P.emit(POOL, lambda e: e.tensor_tensor(out=rs, in0=rs, in1=mhalf[:, 0:1], op=ALU.pow), list(reads_writes) + [CONST], reads_writes)

    for t_, d_ in ((ident[:], c_ident), (tri[:], c_tri), (tri2[:], c_tri2), (tril[:], c_tril), (chs[:], c_ch),
                   (iota16[:, 0, :], c_iota16), (thr16[:, 0, :], c_thr16), (k1s[:], k1T), (k2s[:], k2T),
                   (bcol[:], sgu_bT),
                   (vgb[:], sgu_ln_g.to_broadcast([128, 1024])), (vbb[:], sgu_ln_b.to_broadcast([128, 1024])),
                   (gnb[:], hgrn_gn.to_broadcast([128, 1024])),
                   (lbb[:], lb_logits[0:1, :].to_broadcast([128, 1024])),
                   (GAf[:, 3072:4096], lb_logits[1:2, :].to_broadcast([128, 1024]))):
        dma(SP, t_, d_, dC, [], [])
    swt = GAf[:, 0:1024].rearrange("p (g s) -> p g s", g=8)
    dma(SP, swt, sgu_w.rearrange("g t s -> t g s"), dC, [], [])
    CONST.w = (dC, dC.count)
    dlt = GAf[:, 2048:3072]
    tt(dlt, lbb[:], GAf[:, 3072:4096], ALU.subtract, [CONST], [CONST])
    act(lbb[:], dlt, AF.Sigmoid, [CONST], [CONST])
    for g in range(8):
        tt(swt[:, g, :], swt[:, g, :], tril[:], ALU.mult, [CONST], [CONST])
    for g in range(8):
        pa, pb = psq(2 + (g % 2), g // 2 % 4)
        tp(pa, swt[:, g, :], [CONST], pb)
        vcopy(WT[:, g, :], pa, pb, [CONST])
    for b_ in Gsb:
        b_.w = CONST.w
        b_.r = list(CONST.r)
    P.emit(DVE, lambda e: e.memset(mhalf[:], -0.5), [], [CONST])
    P.emit(DVE, lambda e: e.memset(Sf[:], 0.0), [], Sfb)
    qeT0 = [sb([128, 128], BF16, "qeT0") for _ in range(2)]
    qeT1 = [sb([128, 128], BF16, "qeT1") for _ in range(2)]
    qeTb = [Buf(), Buf()]
    for i in range(2):
        P.emit(DVE, lambda e, i=i: e.memset(qeT0[i][:], 0.0), [], [qeTb[i]])
        P.emit(DVE, lambda e, i=i: e.memset(qeT1[i][:], 0.0), [], [qeTb[i]])

    wlist = []

    def wq_add(ap, ncols):
        wlist.append((ap, ncols))
        return len(wlist) - 1

    wstate = {"issued": 0}

    def w_issue(i):
        ap, ncols = wlist[i]
        s = i % NW
        dma(POOL, Wt[s][:, :, 0:ncols], ap.rearrange("(kc p) n -> p kc n", p=128), Wd[s], [], [Wb[s]])

    def w_acquire(i):
        while wstate["issued"] <= min(i + NW - 1, len(wlist) - 1):
            w_issue(wstate["issued"])
            wstate["issued"] += 1
            if wstate.get("trickle") and wstate["issued"] % 3 == 0:
                precast_chunks(1)
        s = i % NW
        return Wt[s], Wb[s]

    plan = {}
    if need_kv:
      plan["kv_k"] = [wq_add(xa_wk[:, n * 512:(n + 1) * 512], 512) for n in range(4)]
      plan["kv_v"] = [wq_add(xa_wv[:, n * 512:(n + 1) * 512], 512) for n in range(4)]
    if need_pre:
      plan["pre_i"] = [] if skip_pre else [wq_add(w_fi[:, n * 512:(n + 1) * 512], 512) for n in (2, 3)]
    plan["pass"] = []
    for p_ in range(npass if need_main else 0):
        d = {}
        d["sgu"] = [wq_add(w_sgu[:, g * 256:(g + 1) * 256], 256) for g in range(8)]
        d["hg"] = [wq_add(w_hg[:, h * 512:(h + 1) * 512], 512) for h in range(8)]
        d["wo1"] = [wq_add(w_out[:, n * 512:(n + 1) * 512], 512) for n in range(4)]
        if stage >= 2:
            d["xq"] = [wq_add(xa_wq[:, n * 512:(n + 1) * 512], 512) for n in range(4)]
            d["xo"] = [wq_add(xa_wo[:, n * 512:(n + 1) * 512], 512) for n in range(4)]
        if stage >= 3:
            d["pq"] = [wq_add(peer_wq[:, n * 512:(n + 1) * 512], 512) for n in range(4)]
        plan["pass"].append(d)

    def proj_tok(ps_ap, ps_bufs, Xt, Xbuf, t, Wtile, Wbuf, ncols, c0=0):
        for kc in range(KC):
            mm(ps_ap, Xt[:, kc, t * 128:(t + 1) * 128], Wtile[:, kc, c0:c0 + ncols], kc == 0, kc == KC - 1,
               [Xbuf, Wbuf], ps_bufs)

    def finish():
        for t in range(NT):
            dma(SP, out[t * 128:(t + 1) * 128, :], R[:, t, :], dST[t], [Rb[t]], [])
        for t in range(NT):
            nc.sync.wait_ge(dST[t].h, dST[t].count)
        return nc, P, declared

    if stop == "setup":
        P.emit(DVE, lambda e: e.memset(R[:], 1.0), [], Rb)
        for t in range(NT):
            vcopy(R[:, t, 0:1024], WT[:].rearrange("p a b -> p (a b)"), [CONST], [Rb[t]])
            vcopy(R[:, t, 1024:2048], lbb[:], [CONST], [Rb[t]])
        return finish()

    dma(POOL, A[:, :, 0:256], memT.rearrange("(kc p) m -> p kc m", p=128), dA, [], Ab)
    for n in range(4):
        Wtile, Wbuf = w_acquire(plan["kv_k"][n])
        for c in range(4):
            pa, pb = psq(c % 2, 0, 2)
            for kc in range(KC):
                mm(pa, Wtile[:, kc, c * 128:(c + 1) * 128], A[:, kc, 0:256], kc == 0, kc == KC - 1,
                   [Wbuf] + Ab, pb)
            evac(kT[:, n * 4 + c, :], pa, pb, [kTb])
    for n in range(4):
        Wtile, Wbuf = w_acquire(plan["kv_v"][n])
        for mc in range(2):
            pa, pb = psq(mc, 0, 4)
            for kc in range(KC):
                mm(pa, A[:, kc, mc * 128:(mc + 1) * 128], Wtile[:, kc, :], kc == 0, kc == KC - 1,
                   [Wbuf] + Ab, pb)
            evac(vM[:, mc, n * 512:(n + 1) * 512], pa, pb, [vMb])

    if stop == "kv":
        for t in range(NT):
            vcopy(R[:, t, :], vM[:, t, :], [vMb], [Rb[t]])
        vcopy(R[:, 0, 0:256], kT[:, 3, :], [kTb], [Rb[0]])
        return finish()

    r128 = {}
    ARENA_W = 10752
    arena = sb([128, ARENA_W], F32, "arena")
    arena_off = [0]

    def stage_begin():
        P.barrier()
        r128.clear()
        arena_off[0] = 0

    class ARing:
        def __init__(self, shape, dt, n):
            per = 1
            for d_ in shape[1:]:
                per *= d_
            esz = 2 if dt == BF16 else 4
            words = (per * esz + 3) // 4
            words = (words + 7) // 8 * 8
            self.t = []
            for _ in range(n):
                o = arena_off[0]
                assert o + words <= ARENA_W, ("arena overflow", o, words)
                v = arena[:, o:o + words]
                if dt != F32:
                    v = v.bitcast(dt)
                v = v[:, 0:per]
                if len(shape) == 3:
                    v = v.rearrange("p (a b) -> p a b", a=shape[1])
                self.t.append(v)
                arena_off[0] = o + words
            self.b = [Buf() for _ in range(n)]
            self.i = 0

        def get(self):
            k = self.i % len(self.t)
            self.i += 1
            return self.t[k], self.b[k]

    def T(name, dt=F32, shape=(128, 128), n=2):
        key = (name, dt, tuple(shape))
        if key not in r128:
            r128[key] = ARing(shape, dt, n)
        return r128[key].get()

    def hgrn_gate(pf, pfb, h):
        hs = slice(h * 128, (h + 1) * 128)
        sig, sigb = T("sig")
        sigmoid_le(sig[:], pf, pfb, sigb)
        f, fb = T("f")
        ts(f[:], sig[:], -1.0, 1.0, ALU.mult, ALU.add, [sigb], [fb])
        tt(f[:], f[:], lbb[:, hs], ALU.mult, [fb, CONST], [fb])
        tt(f[:], f[:], sig[:], ALU.add, [fb, sigb], [fb])
        lf, lfb = T("lf")
        act(lf[:], f[:], AF.Ln, [fb], [lfb])
        kk, kkb = T("kk")
        ts(kk[:], f[:], -1.0, 1.0, ALU.mult, ALU.add, [fb], [kkb])
        return lf, lfb, kk, kkb

    def hgrn_state(h, lf, lfb, kk, kkb, ib, ibb, bx, by, u):
        pR, pRb = psq(bx, 1)
        mm(pR, tri2[:], lf[:], True, True, [CONST, lfb], pRb)
        pL, pLb = psq(bx, 2)
        mm(pL[:, 0:2], lf[:], chs[:], True, True, [CONST, lfb], pLb)
        er, erb = T("er")
        act(er[:], pR, AF.Exp, pRb, [erb])
        eal, ealb = T("eal", F32, (128, 2))
        act(eal[:], pL[:, 0:2], AF.Exp, pLb, [ealb])
        kd0, kd0b = T("kd0", BF16)
        stt(kd0[:], kk[:], chs[:, 0:1], er[:], ALU.mult, ALU.mult, [kkb, erb, CONST], [kd0b])
        kd1, kd1b = T("kd1", BF16)
        stt(kd1[:], kk[:], chs[:, 1:2], er[:], ALU.mult, ALU.mult, [kkb, erb, CONST], [kd1b])
        acopy(Sb0[u][:, h, :], Sf[:, h, :], [Sfb[h]], [Sb0b[u][h]])
        pU0, pU0b = psq(by, 0)
        mm(pU0, kd0[:], ib[:], True, True, [kd0b, ibb], pU0b)
        stt(Sf[:, h, :], Sf[:, h, :], eal[:, 0:1], pU0, ALU.mult, ALU.add, [Sfb[h], ealb] + pU0b, [Sfb[h]])
        acopy(Sb1[u][:, h, :], Sf[:, h, :], [Sfb[h]], [Sb1b[u][h]])
        pU1, pU1b = psq(by, 1)
        mm(pU1, kd1[:], ib[:], True, True, [kd1b, ibb], pU1b)
        stt(Sf[:, h, :], Sf[:, h, :], eal[:, 1:2], pU1, ALU.mult, ALU.add, [Sfb[h], ealb] + pU1b, [Sfb[h]])

    ucount = [0]

    def hgrn_unit(h, t, pa, pb, u):
        hs = slice(h * 128, (h + 1) * 128)
        bx, by, bz = (2, 3, 4) if u == 0 else (5, 6, 7)
        lf, lfb, kk, kkb = hgrn_gate(pa[:, 128:256], pb, h)
        ib, ibb = T("ib", BF16)
        acopy(ib[:], pa[:, 256:384], pb, [ibb])
        sg, sgb = T("sg")
        sigmoid_le(sg[:], pa[:, 384:512], pb, sgb)
        tt(sg[:], pa[:, 384:512], sg[:], ALU.mult, pb + [sgb], [sgb])
        yield
        pA, pAb = psq(bx, 0)
        mm(pA, tri[:], lf[:], True, True, [CONST, lfb], pAb)
        ea, eab = T("ea")
        act(ea[:], pA, AF.Exp, pAb, [eab])
        ena, enab = T("ena")
        act(ena[:], pA, AF.Exp, pAb, [enab], scale=-1.0)
        qe, qeb = T("qe")
        tt(qe[:], pa[:, 0:128], ea[:], ALU.mult, pb + [eab], [qeb])
        ke, keb = T("ke")
        tt(ke[:], kk[:], ena[:], ALU.mult, [kkb, enab], [keb])
        yield
        hgrn_state(h, lf, lfb, kk, kkb, ib, ibb, bx, by, u)
        yield
        pT1, pT1b = psq(bz, 0)
        tp(pT1, qe[:], [qeb], pT1b)
        pT2, pT2b = psq(bz, 1)
        tp(pT2, ke[:], [keb], pT2b)
        qeT, qeTfb = T("qeT", BF16)
        acopy(qeT[:], pT1, pT1b, [qeTfb])
        vcopy(qeT0[u][:, 0:64], pT1[:, 0:64], pT1b, [qeTb[u]])
        vcopy(qeT1[u][:, 64:128], pT1[:, 64:128], pT1b, [qeTb[u]])
        keT, keTb = T("keT", BF16)
        acopy(keT[:], pT2, pT2b, [keTb])
        yield
        pS, pSb = psq(bz, 2)
        mm(pS, keT[:], qeT[:], True, True, [keTb, qeTfb], pSb)
        scT, scTb = T("scT", BF16)
        tt(scT[:], pS, tri[:], ALU.mult, pSb + [CONST], [scTb])
        yield
        pO, pOb = psq(bz, 3)
        mm(pO, scT[:], ib[:], True, False, [scTb, ibb], pOb)
        mm(pO, qeT0[u][:], Sb0[u][:, h, :], False, False, [qeTb[u], Sb0b[u][h]], pOb)
        mm(pO, qeT1[u][:], Sb1[u][:, h, :], False, True, [qeTb[u], Sb1b[u][h]], pOb)
        yield
        junk, junkb = T("junk")
        ss, ssb = T("ss", F32, (128, 1))
        act(junk[:], pO, AF.Square, pOb, [junkb, ssb], accum_out=ss[:])
        rstd_le(ss[:], [ssb], 1.0 / 128.0, LN_EPS)
        t1, t1b = T("t1")
        stt(t1[:], pO, ss[:, 0:1], gnb[:, hs], ALU.mult, ALU.mult, pOb + [ssb, CONST], [t1b])
        yb, ybb = T("yb")
        tt(yb[:], t1[:], sg[:], ALU.mult, [t1b, sgb], [ybb])
        yield
        pT3, pT3b = psq(by, 2)
        tp(pT3, yb[:], [ybb], pT3b)
        evac(B[:, 8 + h, t * 128:(t + 1) * 128], pT3, pT3b, [Bb[t]])

    def run_interleaved(starters, width=2):
        active = []
        for st_ in starters:
            active.append(st_())
            while len(active) >= width:
                oldest = active[0]
                for g_ in list(active):
                    try:
                        next(g_)
                    except StopIteration:
                        active.remove(g_)
                if oldest not in active:
                    break
        while active:
            for g_ in list(active):
                try:
                    next(g_)
                except StopIteration:
                    active.remove(g_)

    scount = [0]

    def sgu_unit(g, t, pa, pb, u):
        gs = slice(g * 128, (g + 1) * 128)
        gu, gub = T("gu")
        act(gu[:], pa[:, 0:128], AF.Gelu_apprx_tanh, pb, [gub])
        gv, gvb = T("gv")
        act(gv[:], pa[:, 128:256], AF.Gelu_apprx_tanh, pb, [gvb])
        st, stb = T("bst", F32, (128, 6))
        yield
        P.emit(DVE, lambda e: e.bn_stats(out=st[:], in_=gv[:]), [gvb], [stb])
        mv, mvb = T("bmv", F32, (128, 2))
        P.emit(DVE, lambda e: e.bn_aggr(out=mv[:], in_=st[:]), [stb], [mvb])
        rs, rsb = T("brs", F32, (128, 1))
        ts(rs[:], mv[:, 1:2], LN_EPS, None, ALU.add, None, [mvb], [rsb])
        rstd_from(rs[:], [rsb])
        yield
        vn, vnb = T("vn")
        ts(vn[:], gv[:], mv[:, 0:1], rs[:, 0:1], ALU.subtract, ALU.mult, [gvb, mvb, rsb], [vnb])
        tt(vn[:], vn[:], vgb[:, gs], ALU.mult, [vnb, CONST], [vnb])
        vb16, vb16b = T("vb16", BF16)
        tt(vb16[:], vn[:], vbb[:, gs], ALU.add, [vnb, CONST], [vb16b])
        yield
        pZ, pZb = psq(2 + 3 * u, 0)
        mm(pZ, WT[:, g, :], vb16[:], True, True, [CONST, vb16b], pZb)
        ya, yab = T("ya")
        stt(ya[:], pZ, bcol[:, g:g + 1], gu[:], ALU.add, ALU.mult, pZb + [CONST, gub], [yab])
        yield
        pT_, pTb = psq(3 + 3 * u, 0)
        tp(pT_, ya[:], [yab], pTb)
        evac(B[:, g, t * 128:(t + 1) * 128], pT_, pTb, [Bb[t]])

    NWD = 2

    WF = GAf[:, :].bitcast(BF16).rearrange("p (kc n) -> p kc n", kc=KC)
    wfb = Buf()
    wfb.w = CONST.w
    wfb.r = list(CONST.r)
    dWF = DSem(nc, "dWF")
    if need_pre and not skip_pre:
        dma(POOL, WF, w_fi[:, 0:1024].rearrange("(kc p) n -> p kc n", p=128), dWF, [], [wfb])

    def prescan_group(grp):
        for blk in range(4):
            if blk < 2:
                Wtile, Wbuf, c0 = WF, wfb, blk * 512
            else:
                ii_ = plan["pre_i"][blk - 2]
                Wtile, Wbuf, c0 = Wt[ii_ % NW], Wb[ii_ % NW], 0
            for t in range(NT):
                bank = (0 if blk < 2 else 4) + 2 * t + (blk % 2)
                pa, pb = psq(bank, 0, 4)
                proj_tok(pa, pb, A, Ab[t], t, Wtile, Wbuf, 512, c0)
        for t in range(NT):
            fb0, ib0 = 2 * t, 4 + 2 * t
            pF = PSall[:, fb0 * 512:(fb0 + 2) * 512]
            pFb = [PSb[fb0][0], PSb[fb0 + 1][0]]
            pI = PSall[:, ib0 * 512:(ib0 + 2) * 512]
            pIb = [PSb[ib0][0], PSb[ib0 + 1][0]]
            sig, sigb = T("wsig", F32, (128, 1024), NWD)
            sigmoid_le(sig, pF, pFb, sigb)
            f, fb = T("wf", F32, (128, 1024), NWD)
            ts(f, sig, -1.0, 1.0, ALU.mult, ALU.add, [sigb], [fb])
            tt(f, f, lbb[:], ALU.mult, [fb, CONST], [fb])
            tt(f, f, sig, ALU.add, [fb, sigb], [fb])
            act(sig, f, AF.Ln, [fb], [sigb])
            ts(f, f, -1.0, 1.0, ALU.mult, ALU.add, [fb], [fb])
            ib, ibb = T("wib", BF16, (128, 1024), NWD)
            acopy(ib, pI, pIb, [ibb])
            for hf in range(2):
                mm(pF[:, hf * 512:(hf + 1) * 512], tri2[:], sig[:, hf * 512:(hf + 1) * 512], True, True, [CONST, sigb], [pFb[hf]])
            er, erb = T("wer", F32, (128, 1024), NWD)
            act(er, pF, AF.Exp, pFb, [erb])
            for h in range(8):
                mm(pI[:, 2 * h:2 * h + 2], sig[:, h * 128:(h + 1) * 128], chs[:], True, True, [CONST, sigb], [pIb[0]])
            eal, ealb = T("weal", F32, (128, 8, 2), 2)
            act(eal.rearrange("p a b -> p (a b)"), pI[:, 0:16], AF.Exp, [pIb[0]], [ealb])
            kd0, kd0b = T("wkd0", BF16, (128, 1024), NWD)
            stt(kd0, f, chs[:, 0:1], er, ALU.mult, ALU.mult, [fb, erb, CONST], [kd0b])
            kd1, kd1b = T("wkd1", BF16, (128, 1024), NWD)
            stt(kd1, f, chs[:, 1:2], er, ALU.mult, ALU.mult, [fb, erb, CONST], [kd1b])
            Sf3 = Sf[:]
            for h in range(8):
                hs = slice(h * 128, (h + 1) * 128)
                mm(pF[:, hs], kd0[:, hs], ib[:, hs], True, True, [kd0b, ibb], [pFb[h // 4]])
            tt(Sf3, Sf3, eal[:, :, 0:1].to_broadcast([128, 8, 128]), ALU.mult, Sfb + [ealb], Sfb)
            tt(Sf3, Sf3, pF.rearrange("p (a b) -> p a b", a=8), ALU.add, Sfb + pFb, Sfb)
            for h in range(8):
                hs = slice(h * 128, (h + 1) * 128)
                mm(pI[:, hs], kd1[:, hs], ib[:, hs], True, True, [kd1b, ibb], [pIb[h // 4]])
            tt(Sf3, Sf3, eal[:, :, 1:2].to_broadcast([128, 8, 128]), ALU.mult, Sfb + [ealb], Sfb)
            tt(Sf3, Sf3, pI.rearrange("p (a b) -> p a b", a=8), ALU.add, Sfb + pIb, Sfb)

    if need_pre and not skip_pre:
        assert wstate["issued"] <= plan["pre_i"][-1] + 1, (wstate["issued"], plan["pre_i"])
        while wstate["issued"] <= plan["pre_i"][-1]:
            w_issue(wstate["issued"])
            wstate["issued"] += 1
    for grp in range(0 if skip_pre else NPRE):
        precast_chunks(2 if grp < 4 else 1)
        dma(POOL, A[:, :, :], xprevT[:, grp * TP:(grp + 1) * TP].rearrange("(kc p) t -> p kc t", p=128), dA, [], Ab)
        prescan_group(grp)
    for b_ in Gsb:
        if wfb.w is not None:
            b_.r.append(wfb.w)
        b_.r.extend(wfb.r)

    if stop == "prescan":
        for t in range(NT):
            vcopy(R[:, t, 0:1024], Sf[:].rearrange("p a b -> p (a b)"), Sfb, [Rb[t]])
            vcopy(R[:, t, 1024:2048], Sf[:].rearrange("p a b -> p (a b)"), Sfb, [Rb[t]])
        return finish()

    def load_ln(i):
        dma(SP, lng[:], ln_gb[2 * i].to_broadcast([128, D]), dLN, [], [lngb])
        dma(SP, lnb[:], ln_gb[2 * i + 1].to_broadcast([128, D]), dLN, [], [lnbb])
        lngb.w = (dLN, dLN.count)
        lnbb.w = (dLN, dLN.count)

    def layer_norm_tile(t):
        st, stb = T("lst", F32, (128, 4, 6))
        for c in range(4):
            P.emit(DVE, lambda e, c=c: e.bn_stats(out=st[:, c, :], in_=R[:, t, c * 512:(c + 1) * 512]), [Rb[t]], [stb])
        mv, mvb = T("lmv", F32, (128, 2))
        P.emit(DVE, lambda e: e.bn_aggr(out=mv[:], in_=st[:].rearrange("p a b -> p (a b)")), [stb], [mvb])
        rs, rsb = T("lrs", F32, (128, 1))
        act(rs[:], mv[:, 1:2], AF.Ln, [mvb], [rsb], bias=LN_EPS)
        act(rs[:], rs[:], AF.Exp, [rsb], [rsb], scale=-0.5)
        ts(R[:, t, :], R[:, t, :], mv[:, 0:1], rs[:, 0:1], ALU.subtract, ALU.mult, [Rb[t], mvb, rsb], [Rb[t]])
        tt(R[:, t, :], R[:, t, :], lng[:], ALU.mult, [Rb[t], lngb], [Rb[t]])
        tt(R[:, t, :], R[:, t, :], lnb[:], ALU.add, [Rb[t], lnbb], [Rb[t]])

    def transpose_R_to_A(t):
        for q4 in range(4):
            bank = 2 + (q4 % 2)
            pa, pb = psq(bank, 0, 4)
            for j in range(4):
                kc = q4 * 4 + j
                tp(pa[:, j * 128:(j + 1) * 128], R[:, t, kc * 128:(kc + 1) * 128], [Rb[t]], [pb[j]])
            evac(A[:, q4 * 4:(q4 + 1) * 4, t * 128:(t + 1) * 128], pa.rearrange("p (a b) -> p a b", a=4), pb, [Ab[t]])

    def out_proj_residual(wids, first):
        for n in range(4):
            Wtile, Wbuf = w_acquire(wids[n])
            for t in range(NT):
                pa, pb = psq((n * NT + t) % 2, 0, 4)
                proj_tok(pa, pb, B, Bb[t], t, Wtile, Wbuf, 512)
                stt(R[:, t, n * 512:(n + 1) * 512], R[:, t, n * 512:(n + 1) * 512], ALPHA, pa, ALU.mult, ALU.add,
                    [Rb[t]] + pb, [Rb[t]])

    for ps_ in range(npass):
        pl = plan["pass"][ps_]
        t0 = ps_ * TP
        stage_begin()
        dma(POOL, A[:, :, :], xT[:, t0:t0 + TP].rearrange("(kc p) t -> p kc t", p=128), dA, [], Ab)
        for t in range(NT):
            dma(SP, R[:, t, :], x_tok[t0 + t * 128:t0 + (t + 1) * 128, :], dXR[t], [], [Rb[t]])
        load_ln(0)
        if ps_ == 0:
            wstate["trickle"] = True
        if dbg == "p_load":
            return finish()
        ucnt = [0]

        def sgu_start(g, t):
            def go():
                Wtile, Wbuf = w_acquire(pl["sgu"][g])
                u = ucnt[0] % 2
                ucnt[0] += 1
                pa, pb = psq(u, 0, 2)
                proj_tok(pa, pb, A, Ab[t], t, Wtile, Wbuf, 256)
                return sgu_unit(g, t, pa, pb, u)
            return go

        run_interleaved([sgu_start(g, t) for g in range(8) for t in range(NT)])
        if dbg == "p_sgu":
            return finish()

        def hg_start(h, t):
            def go():
                Wtile, Wbuf = w_acquire(pl["hg"][h])
                u = ucnt[0] % 2
                ucnt[0] += 1
                pa, pb = psq(u, 0, 4)
                proj_tok(pa, pb, A, Ab[t], t, Wtile, Wbuf, 512)
                return hgrn_unit(h, t, pa, pb, u)
            return go

        run_interleaved([hg_start(h, t) for h in range(8) for t in range(NT)])
        if dbg in ("p_hg", "hg1", "hg2", "hg3"):
            return finish()
        out_proj_residual(pl["wo1"], True)
        if dbg == "p_wo":
            return finish()
        for t in range(NT):
            layer_norm_tile(t)
            if stage >= 2:
                transpose_R_to_A(t)
        if stage >= 2:
            stage_begin()
            load_ln(1)
            for h in range(4):
                Wtile, Wbuf = w_acquire(pl["xq"][h])
                qT, qTb = T("qT", BF16, (128, 4, TP), 2)
                for c in range(4):
                    pa, pb = psq(c % 2, 0, NT)
                    for kc in range(KC):
                        mm(pa, Wtile[:, kc, c * 128:(c + 1) * 128], A[:, kc, :], kc == 0, kc == KC - 1,
                           [Wbuf] + Ab, pb)
                    evac(qT[:, c, :], pa, pb, [qTb])
                for t in range(NT):
                    pS, pSb = psq(4, 2 * (t % 2), 2)
                    for c in range(4):
                        mm(pS, qT[:, c, t * 128:(t + 1) * 128], kT[:, h * 4 + c, :], c == 0, c == 3, [qTb, kTb], pSb)
                    mx, mxb = T("mx", F32, (128, 1))
                    P.emit(DVE, lambda e: e.tensor_reduce(out=mx[:], in_=pS, axis=AX.X, op=ALU.max), pSb, [mxb])
                    ts(mx[:], mx[:], -XA_SCALE, None, ALU.mult, None, [mxb], [mxb])
                    pe_, peb = T("pexp", F32, (128, 256))
                    sm, smb = T("sm", F32, (128, 1))
                    act(pe_[:], pS, AF.Exp, pSb + [mxb], [peb, smb], bias=mx[:, 0:1], scale=XA_SCALE, accum_out=sm[:])
                    P.emit(DVE, lambda e: e.reciprocal(out=sm[:], in_=sm[:]), [smb], [smb])
                    pTT, pTTb = psq(5, 2 * (t % 2), 2)
                    for mc in range(2):
                        tp(pTT[:, mc * 128:(mc + 1) * 128], pe_[:, mc * 128:(mc + 1) * 128], [peb], [pTTb[mc]])
                    pT16, pT16b = T("pT16", BF16, (128, 2, 128))
                    evac(pT16[:], pTT.rearrange("p (a b) -> p a b", a=2), pTTb, [pT16b])
                    pO, pOb = psq(6 + (t % 2), 0, 4)
                    for mc in range(2):
                        mm(pO, pT16[:, mc, :], vM[:, mc, h * 512:(h + 1) * 512], mc == 0, mc == 1, [pT16b, vMb], pOb)
                    oh, ohb = T("oh", F32, (128, 512))
                    ts(oh[:], pO, sm[:, 0:1], None, ALU.mult, None, pOb + [smb], [ohb])
                    pTo, pTob = psq(2 + (t % 2), 0, 4)
                    for j in range(4):
                        tp(pTo[:, j * 128:(j + 1) * 128], oh[:, j * 128:(j + 1) * 128], [ohb], [pTob[j]])
                    evac(B[:, h * 4:(h + 1) * 4, t * 128:(t + 1) * 128], pTo.rearrange("p (a b) -> p a b", a=4), pTob, [Bb[t]])
            out_proj_residual(pl["xo"], False)
            for t in range(NT):
                layer_norm_tile(t)
                if stage >= 3:
                    transpose_R_to_A(t)
        if stage >= 3:
            stage_begin()
            load_ln(2)
            if ps_ == 0:
                precast_chunks(NPC)
                for b_ in Gsb:
                    b_.r.append((dPC, dPC.count))
            peer_stage(nc, P, locals())
        for t in range(NT):
            dma(SP, out[t0 + t * 128:t0 + (t + 1) * 128, :], R[:, t, :], dST[t], [Rb[t]], [])

    for t in range(NT):
        nc.sync.wait_ge(dST[t].h, dST[t].count)
    return nc, P, declared


def peer_stage(nc, P, L):
    PE, DVE, ACT, POOL, SP = P.pe, P.dve, P.act, P.pool, P.sp
    T = L["T"]; psq = L["psq"]; mm = L["mm"]; act = L["act"]; tt = L["tt"]; ts = L["ts"]; stt = L["stt"]
    vcopy = L["vcopy"]; acopy = L["acopy"]; evac = L["evac"]; w_acquire = L["w_acquire"]
    A = L["A"]; Ab = L["Ab"]; R = L["R"]; Rb = L["Rb"]; CONST = L["CONST"]
    k1s = L["k1s"]; k2s = L["k2s"]; iota16 = L["iota16"]; thr16 = L["thr16"]; ident = L["ident"]
    pl = L["pl"]; layer_norm_tile = L["layer_norm_tile"]
    sb = L["sb"]

    idx, idxb, gates, gatesb = L["pidx"], L["pidxb"], L["pgates"], L["pgatesb"]

    def route(t, h, pS, pSb):
        s12, s12b = T("s12", F32, (128, 256))
        acopy(s12[:], pS, pSb, [s12b])
        v1, v1b = T("v1", F32, (128, 16, 1))
        v2, v2b = T("v2", F32, (128, 1, 16))
        i12u, i12ub = T("i12u", U32, (128, 32))
        s12m, s12mb = T("s12m", F32, (128, 256))
        for half, vt, vb_ in ((0, v1[:, :, 0], v1b), (1, v2[:, 0, :], v2b)):
            sl = slice(half * 128, (half + 1) * 128)
            P.emit(DVE, lambda e: e.max(out=vt[:, 0:8], in_=s12[:, sl]), [s12b], [vb_])
            P.emit(DVE, lambda e: e.match_replace(out=s12m[:, sl], in_to_replace=vt[:, 0:8], in_values=s12[:, sl], imm_value=NEG),
                   [s12b, vb_], [s12mb])
            P.emit(DVE, lambda e: e.max(out=vt[:, 8:16], in_=s12m[:, sl]), [s12mb, vb_], [vb_])
            P.emit(DVE, lambda e: e.max_index(out=i12u[:, half * 16:half * 16 + 8], in_max=vt[:, 0:8], in_values=s12[:, sl]),
                   [s12b, vb_], [i12ub])
            P.emit(DVE, lambda e: e.max_index(out=i12u[:, half * 16 + 8:half * 16 + 16], in_max=vt[:, 8:16], in_values=s12[:, sl]),
                   [s12b, vb_], [i12ub])
        i1f, i1fb = T("i1f", F32, (128, 1, 16))
        i2f, i2fb = T("i2f", F32, (128, 1, 16))
        vcopy(i1f[:, 0, :], i12u[:, 0:16], [i12ub], [i1fb])
        vcopy(i2f[:, 0, :], i12u[:, 16:32], [i12ub], [i2fb])
        cand, candb = T("cand", F32, (128, 256))
        tt(cand[:].rearrange("p (a b) -> p a b", a=16), v1[:].to_broadcast([128, 16, 16]), v2[:].to_broadcast([128, 16, 16]),
           ALU.add, [v1b, v2b], [candb])
        tsv, tsb = T("tsv", F32, (128, 16))
        candm, candmb = T("candm", F32, (128, 256))
        posu, posub = T("posu", U32, (128, 16))
        P.emit(DVE, lambda e: e.max(out=tsv[:, 0:8], in_=cand[:]), [candb], [tsb])
        P.emit(DVE, lambda e: e.match_replace(out=candm[:], in_to_replace=tsv[:, 0:8], in_values=cand[:], imm_value=NEG),
               [candb, tsb], [candmb])
        P.emit(DVE, lambda e: e.max(out=tsv[:, 8:16], in_=candm[:]), [candmb, tsb], [tsb])
        P.emit(DVE, lambda e: e.max_index(out=posu[:, 0:8], in_max=tsv[:, 0:8], in_values=cand[:]), [candb, tsb], [posub])
        P.emit(DVE, lambda e: e.max_index(out=posu[:, 8:16], in_max=tsv[:, 8:16], in_values=cand[:]), [candb, tsb], [posub])
        nb, nbb = T("nb", F32, (128, 1))
        ts(nb[:], tsv[:, 0:1], -1.0, None, ALU.mult, None, [tsb], [nbb])
        ex, exb = T("ex", F32, (128, 16))
        sm, smb = T("psm", F32, (128, 1))
        act(ex[:], tsv[:], AF.Exp, [tsb, nbb], [exb, smb], bias=nb[:, 0:1], scale=1.0, accum_out=sm[:])
        P.emit(DVE, lambda e: e.reciprocal(out=sm[:], in_=sm[:]), [smb], [smb])
        ts(gates[t][:, h * 16:(h + 1) * 16], ex[:], sm[:, 0:1], None, ALU.mult, None, [exb, smb], [gatesb[t]])
        posf, posfb = T("posf", F32, (128, 16, 1))
        vcopy(posf[:, :, 0], posu[:], [posub], [posfb])
        oh, ohb = T("ohot", F32, (128, 16, 16))
        tt(oh[:], posf[:].to_broadcast([128, 16, 16]), thr16[:].to_broadcast([128, 16, 16]), ALU.is_ge, [posfb, CONST], [ohb])
        af, afb = T("af", F32, (128, 16, 1))
        P.emit(DVE, lambda e: e.tensor_reduce(out=af[:, :, 0], in_=oh[:], axis=AX.X, op=ALU.add), [ohb], [afb])
        bf_, bfb = T("bf", F32, (128, 16, 1))
        stt(bf_[:, :, 0], af[:, :, 0], -16.0, posf[:, :, 0], ALU.mult, ALU.add, [afb, posfb], [bfb])
        e12 = []
        for sel, selb, srcf, srcfb in ((af, afb, i1f, i1fb), (bf_, bfb, i2f, i2fb)):
            oh2, oh2b = T("ohot", F32, (128, 16, 16))
            tt(oh2[:], sel[:].to_broadcast([128, 16, 16]), iota16[:].to_broadcast([128, 16, 16]), ALU.is_equal,
               [selb, CONST], [oh2b])
            tt(oh2[:], oh2[:], srcf[:].to_broadcast([128, 16, 16]), ALU.mult, [oh2b, srcfb], [oh2b])
            ee, eeb = T("ee", F32, (128, 16))
            P.emit(DVE, lambda e, ee=ee, oh2=oh2: e.tensor_reduce(out=ee[:], in_=oh2[:], axis=AX.X, op=ALU.add), [oh2b], [eeb])
            e12.append((ee, eeb))
        stt(idx[t][:, h * 16:(h + 1) * 16], e12[0][0][:], 128.0, e12[1][0][:], ALU.mult, ALU.add,
            [e12[0][1], e12[1][1]], [idxb[t]])

    for n in range(4):
        Wtile, Wbuf = w_acquire(pl["pq"][n])
        for hh in range(2):
            h = n * 2 + hh
            pSs = [psq(2 + t % 2, 2 * ((t // 2) % 2), 2) for t in range(NT)]
            for part in range(2):
                c = hh * 2 + part
                pa, pb = psq(c % 2, 0, NT)
                for kc in range(KC):
                    mm(pa, Wtile[:, kc, c * 128:(c + 1) * 128], A[:, kc, :], kc == 0, kc == KC - 1, [Wbuf] + Ab, pb)
                qc, qcb = T("qc", F32, (128, TP))
                evac(qc[:], pa, pb, [qcb])
                for t in range(NT):
                    pS, pSb = pSs[t]
                    mm(pS[:, part * 128:(part + 1) * 128], qc[:, t * 128:(t + 1) * 128], (k1s if part == 0 else k2s)[:],
                       True, True, [qcb, CONST], [pSb[part]])
            for t in range(NT):
                route(t, h, pSs[t][0], pSs[t][1])

    uvb = L["uvb"]
    Wt1 = L["Wt"][1][:].rearrange("p a b -> p (a b)")
    Wb1 = L["Wb"][1]
    assert pl["pq"][3] % 2 == 1, "last weight block of the pass must live in slot 1"
    xbufs = [Buf(), Buf()]
    for xb_ in xbufs:
        xb_.w = Wb1.w
        xb_.r = list(Wb1.r)
    Gs = list(L["Gs"]) + [Wt1[:, 0:2 * D], Wt1[:, 2 * D:4 * D]]
    Gsb = list(L["Gsb"]) + xbufs
    Gsd = list(L["Gsd"]) + list(L["Gxd"])
    NG = len(Gs)
    for t in range(NT):
        ffq = [psq(4 + q, 0, 4) for q in range(4)]
        pend = None
        LOOK = NG - 1

        def gather(jj):
            sg_ = jj % NG
            P.emit(POOL, lambda e: e.indirect_dma_start(out=Gs[sg_], out_offset=None, in_=uvb,
                                                       in_offset=bass.IndirectOffsetOnAxis(ap=idx[t][:, jj:jj + 1], axis=0)),
                   [idxb[t]], [Gsb[sg_]], dsem=Gsd[sg_])

        for jj in range(LOOK):
            gather(jj)
        for j in range(128 + 1):
            if pend is not None:
                pj, pvs, pgh, pghb = pend
                dg, dgb = T("dg", BF16, (128, 128), 3)
                ts(dg[:], ident[:], pgh[:, 0:1], gates[t][:, pj:pj + 1], ALU.mult, ALU.mult, [CONST, pghb, gatesb[t]], [dgb])
                for q in range(4):
                    mm(ffq[q][0], dg[:], Gs[pvs][:, D + q * 512:D + (q + 1) * 512], pj == 0, pj == 127, [dgb, Gsb[pvs]], ffq[q][1])
                pend = None
            if j < 128:
                if j + LOOK < 128:
                    gather(j + LOOK)
                s = j % NG
                hd, hdb = T("hdj", F32, (128, 1), 4)
                junk, junkb = T("pjunk", BF16, (128, D), 2)
                P.emit(DVE, lambda e: e.scalar_tensor_tensor(out=junk, in0=R[:, t, :], scalar=1.0, in1=Gs[s][:, 0:D],
                                                             op0=ALU.mult, op1=ALU.mult, accum_out=hd[:]),
                       [Rb[t], Gsb[s]], [junkb, hdb])
                gh, ghb = T("ghj", F32, (128, 1), 4)
                act(gh[:], hd[:], AF.Gelu_apprx_tanh, [hdb], [ghb])
                pend = (j, s, gh, ghb)
        for q in range(4):
            stt(R[:, t, q * 512:(q + 1) * 512], R[:, t, q * 512:(q + 1) * 512], ALPHA, ffq[q][0], ALU.mult, ALU.add,
                [Rb[t]] + ffq[q][1], [Rb[t]])
        layer_norm_tile(t)
    for xb_ in xbufs:
        if xb_.w is not None:
            Wb1.r.append(xb_.w)
        Wb1.r.extend(xb_.r)


def _consts():
    s = np.arange(128)
    same = (s[:, None] // 64) == (s[None, :] // 64)
    c = {}
    c["c_ident"] = np.eye(128, dtype=np.float32)
    c["c_tri"] = ((s[:, None] <= s[None, :]) & same).astype(np.float32)
    c["c_tri2"] = ((s[:, None] > s[None, :]) & same).astype(np.float32)
    c["c_tril"] = (s[:, None] >= s[None, :]).astype(np.float32)
    c["c_ch"] = np.stack([(s < 64), (s >= 64)], axis=1).astype(np.float32)
    c["c_iota16"] = np.tile(np.arange(16, dtype=np.float32)[None, :], (128, 1))
    c["c_thr16"] = np.tile((16.0 * (np.arange(16, dtype=np.float32) + 1.0))[None, :], (128, 1))
    return c


def make_in_maps(inputs):
    f = lambda a: np.ascontiguousarray(np.asarray(a, dtype=np.float32))
    x = f(inputs["x"]); mem = f(inputs["mem"])
    w_in = f(inputs["w_in"])[0]
    MA = 1024
    ua = w_in[:, 0:MA].reshape(D, 8, 128); va = w_in[:, MA:2 * MA].reshape(D, 8, 128)
    w_sgu = np.ascontiguousarray(np.concatenate([ua, va], axis=2).reshape(D, 8 * 256))
    qb, fb, ib, gb = [w_in[:, 2 * MA + i * 1024:2 * MA + (i + 1) * 1024].reshape(D, 8, 128) for i in range(4)]
    w_hg = np.ascontiguousarray(np.concatenate([qb, fb, ib, gb], axis=2).reshape(D, 8 * 512))
    shared = {
        "w_sgu": w_sgu, "w_hg": w_hg, "w_fi": np.ascontiguousarray(w_in[:, 3 * MA:5 * MA]),
        "sgu_w": f(inputs["sgu_w"])[0], "sgu_bT": np.ascontiguousarray(f(inputs["sgu_b"])[0].T),
        "sgu_ln_g": f(inputs["sgu_ln_g"])[0].reshape(1, 1024), "sgu_ln_b": f(inputs["sgu_ln_b"])[0].reshape(1, 1024),
        "lb_logits": f(inputs["hgrn_lb_logits"]), "hgrn_gn": f(inputs["hgrn_norm_g"])[0].reshape(1, 1024),
        "w_out": f(inputs["w_out"])[0], "xa_wq": f(inputs["xa_wq"])[0], "xa_wk": f(inputs["xa_wk"])[0],
        "xa_wv": f(inputs["xa_wv"])[0], "xa_wo": f(inputs["xa_wo"])[0], "peer_wq": f(inputs["peer_wq"])[0],
        "ln1_g": f(inputs["ln1_g"]), "ln1_b": f(inputs["ln1_b"]), "ln2_g": f(inputs["ln2_g"]), "ln2_b": f(inputs["ln2_b"]),
        "ln3_g": f(inputs["ln3_g"]), "ln3_b": f(inputs["ln3_b"]),
        "k1T": np.ascontiguousarray(f(inputs["peer_k1"])[0].T), "k2T": np.ascontiguousarray(f(inputs["peer_k2"])[0].T),
        "peer_uv": np.ascontiguousarray(np.concatenate([f(inputs["peer_u"])[0], f(inputs["peer_v"])[0]], axis=1)),
    }
    shared.update(_consts())
    maps = []
    for c in range(NCORE):
        b, j = c // 4, c % 4
        xs = x[b, j * TOK:(j + 1) * TOK]
        prev = np.zeros((3 * TOK, D), np.float32)
        if j > 0:
            prev[(3 - j) * TOK:] = x[b, 0:j * TOK]
        m = dict(shared)
        m["x_tok"] = np.ascontiguousarray(xs)
        m["xT"] = np.ascontiguousarray(xs.T)
        m["xprevT"] = np.ascontiguousarray(prev.T)
        m["memT"] = np.ascontiguousarray(mem[b].T)
        maps.append(m)
    return maps


def kernel(**inputs):
    nc, _, declared = build_program(3)
    maps = [{k: m[k] for k in declared} for m in make_in_maps(inputs)]
    res = run_bass_kernel_spmd(nc, maps, core_ids=list(range(NCORE)))
    outs = [np.asarray(r["out"]).reshape(TOK, D) for r in res.results]
    full = np.stack([np.concatenate(outs[0:4], axis=0), np.concatenate(outs[4:8], axis=0)], axis=0)
    return full.astype(np.float32)
```
